# Optimizing a Trainium2 kernel written in Bass

```python
import math
import jax, jax.numpy as jnp
from jax import lax
import numpy as np

D_MODEL = 1024
BATCH = 8
SEQ = 2048
DEPTH = 1
DEC_BATCH = 16
DEC_SEQ = 64
PAST_LEN = 2048

CHUNK = 64
Q_BLOCK = 128
HEAD_DIM = 64
DIFF_HEADS = 4
SB_HEADS = 8
DIFF_WIDTH = DIFF_HEADS * 2 * HEAD_DIM
SB_WIDTH = SB_HEADS * HEAD_DIM
MIX_WIDTH = DIFF_WIDTH + SB_WIDTH
IN_WIDTH = 3 * MIX_WIDTH
D_FF = 4 * D_MODEL
EPS = 1e-6
NEG = -1e30

kernel_name = 'hymba_diff_stickbreaking_streaming_step'


def rmsnorm(x, g):
    xf = x.astype(jnp.float32)
    y = xf * lax.rsqrt(jnp.mean(xf * xf, axis=-1, keepdims=True) + EPS)
    return (y * g.astype(jnp.float32)).astype(x.dtype)


def alibi_slopes():
    return jnp.asarray(2.0 ** (-8.0 * np.arange(1, DIFF_HEADS + 1) / DIFF_HEADS), dtype=jnp.float32)


def lambda_init(layer):
    return 0.8 - 0.6 * math.exp(-0.3 * layer)


def modulate(c, w_ada, b_ada):
    m = jax.nn.silu(c) @ w_ada + b_ada
    return jnp.split(m[:, None, :], 6, axis=-1)


def split_proj(u):
    b, t, _ = u.shape
    cuts = [DIFF_WIDTH, 2 * DIFF_WIDTH, 3 * DIFF_WIDTH, 3 * DIFF_WIDTH + SB_WIDTH, 3 * DIFF_WIDTH + 2 * SB_WIDTH]
    qd, kd, vd, qs, ks, vs = jnp.split(u, cuts, axis=-1)
    qd = qd.reshape(b, t, DIFF_HEADS, 2, HEAD_DIM)
    kd = kd.reshape(b, t, DIFF_HEADS, 2 * HEAD_DIM)
    vd = vd.reshape(b, t, DIFF_HEADS, 2 * HEAD_DIM)
    qs = qs.reshape(b, t, SB_HEADS, HEAD_DIM)
    ks = ks.reshape(b, t, SB_HEADS, HEAD_DIM)
    vs = vs.reshape(b, t, SB_HEADS, HEAD_DIM)
    return qd, kd, vd, qs, ks, vs


def mix_queries(qd, qs, q_pos, kd, vd, ks, vs, k_pos, lam, subln_g, lam_init):
    f32 = jnp.float32
    b, tk = kd.shape[0], kd.shape[1]
    tq = qd.shape[1]
    scale = HEAD_DIM ** -0.5
    kd2 = kd.reshape(b, tk, DIFF_HEADS, 2, HEAD_DIM)
    s = jnp.einsum('bqhmd,bkhmd->bhmqk', qd.astype(f32), kd2.astype(f32)) * scale
    dist = jnp.abs(q_pos[:, None] - k_pos[None, :]).astype(f32)
    s = s - alibi_slopes()[None, :, None, None, None] * dist
    visible = (k_pos[None, :] // CHUNK) <= (q_pos[:, None] // CHUNK)
    p = jax.nn.softmax(jnp.where(visible, s, NEG), axis=-1)
    a = p[:, :, 0] - lam * p[:, :, 1]
    od = jnp.einsum('bhqk,bkhe->bqhe', a, vd.astype(f32))
    od = rmsnorm(od, subln_g) * (1.0 - lam_init)
    z = jnp.einsum('bqhd,bkhd->bhqk', qs.astype(f32), ks.astype(f32)) * scale
    earlier = k_pos[None, :] < q_pos[:, None]
    log_keep = jnp.where(earlier, jax.nn.log_sigmoid(-z), 0.0)
    between = lax.cumsum(log_keep, axis=3, reverse=True) - log_keep
    w = jnp.where(earlier, jnp.exp(jax.nn.log_sigmoid(z) + between), 0.0)
    osb = jnp.einsum('bhqk,bkhd->bqhd', w, vs.astype(f32))
    return jnp.concatenate([od.reshape(b, tq, DIFF_WIDTH), osb.reshape(b, tq, SB_WIDTH)], axis=-1)


def prompt_mix(qd, kd, vd, qs, ks, vs, lam, subln_g, lam_init):
    b, s = qd.shape[0], qd.shape[1]
    nblk = s // Q_BLOCK
    k_pos = jnp.arange(s)
    qd_blocks = qd.reshape(b, nblk, Q_BLOCK, DIFF_HEADS, 2, HEAD_DIM).swapaxes(0, 1)
    qs_blocks = qs.reshape(b, nblk, Q_BLOCK, SB_HEADS, HEAD_DIM).swapaxes(0, 1)

    def blk(args):
        i, qd_b, qs_b = args
        q_pos = i * Q_BLOCK + jnp.arange(Q_BLOCK)
        return mix_queries(qd_b, qs_b, q_pos, kd, vd, ks, vs, k_pos, lam, subln_g, lam_init)

    out = lax.map(blk, (jnp.arange(nblk), qd_blocks, qs_blocks))
    return out.swapaxes(0, 1).reshape(b, s, MIX_WIDTH)


def sample_mix(qd, kd, vd, qs, ks, vs, ck_d, cv_d, ck_s, cv_s, lam, subln_g, lam_init):
    past, t = ck_d.shape[1], qd.shape[1]
    q_pos = past + jnp.arange(t)
    k_pos = jnp.arange(past + t)
    kd_all = jnp.concatenate([ck_d.astype(kd.dtype), kd], axis=1)
    vd_all = jnp.concatenate([cv_d.astype(vd.dtype), vd], axis=1)
    ks_all = jnp.concatenate([ck_s.astype(ks.dtype), ks], axis=1)
    vs_all = jnp.concatenate([cv_s.astype(vs.dtype), vs], axis=1)
    return mix_queries(qd, qs, q_pos, kd_all, vd_all, ks_all, vs_all, k_pos, lam, subln_g, lam_init)


def pre(x, g, shift, scale):
    return rmsnorm(x, g) * (1.0 + scale) + shift


def post_add(x, y, g, gate):
    return x + gate * rmsnorm(y, g)


def ffn(h, w_up, w_down):
    return jnp.square(jax.nn.relu(h @ w_up)) @ w_down


def setup_inputs(seed: int = 0) -> dict:
    key = jax.random.key(seed)
    ks = jax.random.split(key, 24)
    f32 = jnp.float32
    nrm = lambda k, shape, s: jax.random.normal(k, shape, f32) * s
    gain = lambda k, shape: 1.0 + 0.1 * jax.random.normal(k, shape, f32)
    return {
        'x_prompt': nrm(ks[0], (BATCH, SEQ, D_MODEL), 1.0),
        'x_sample': nrm(ks[1], (DEC_BATCH, DEC_SEQ, D_MODEL), 1.0),
        'c_prompt': nrm(ks[2], (BATCH, D_MODEL), 1.0),
        'c_sample': nrm(ks[3], (DEC_BATCH, D_MODEL), 1.0),
        'cache_diff_k': nrm(ks[4], (DEPTH, DEC_BATCH, PAST_LEN, DIFF_HEADS, 2 * HEAD_DIM), 1.0),
        'cache_diff_v': nrm(ks[5], (DEPTH, DEC_BATCH, PAST_LEN, DIFF_HEADS, 2 * HEAD_DIM), 1.0),
        'cache_sb_k': nrm(ks[6], (DEPTH, DEC_BATCH, PAST_LEN, SB_HEADS, HEAD_DIM), 1.0),
        'cache_sb_v': nrm(ks[7], (DEPTH, DEC_BATCH, PAST_LEN, SB_HEADS, HEAD_DIM), 1.0),
        'w_ada': nrm(ks[8], (DEPTH, D_MODEL, 6 * D_MODEL), 0.2 * D_MODEL ** -0.5),
        'b_ada': nrm(ks[9], (DEPTH, 6 * D_MODEL), 0.02),
        'g_pre_mix': gain(ks[10], (DEPTH, D_MODEL)),
        'g_post_mix': gain(ks[11], (DEPTH, D_MODEL)),
        'w_in': nrm(ks[12], (DEPTH, D_MODEL, IN_WIDTH), D_MODEL ** -0.5),
        'lambda_q1': nrm(ks[13], (DEPTH, HEAD_DIM), 0.1),
        'lambda_k1': nrm(ks[14], (DEPTH, HEAD_DIM), 0.1),
        'lambda_q2': nrm(ks[15], (DEPTH, HEAD_DIM), 0.1),
        'lambda_k2': nrm(ks[16], (DEPTH, HEAD_DIM), 0.1),
        'diff_subln_g': gain(ks[17], (DEPTH, 2 * HEAD_DIM)),
        'w_out': nrm(ks[18], (DEPTH, MIX_WIDTH, D_MODEL), MIX_WIDTH ** -0.5),
        'g_pre_ffn': gain(ks[19], (DEPTH, D_MODEL)),
        'g_post_ffn': gain(ks[20], (DEPTH, D_MODEL)),
        'w_up': nrm(ks[21], (DEPTH, D_MODEL, D_FF), D_MODEL ** -0.5),
        'w_down': nrm(ks[22], (DEPTH, D_FF, D_MODEL), D_FF ** -0.5),
    }


def reference(x_prompt, x_sample, c_prompt, c_sample, cache_diff_k, cache_diff_v, cache_sb_k, cache_sb_v,
              w_ada, b_ada, g_pre_mix, g_post_mix, w_in, lambda_q1, lambda_k1, lambda_q2, lambda_k2,
              diff_subln_g, w_out, g_pre_ffn, g_post_ffn, w_up, w_down):
    hp, hs = x_prompt, x_sample
    dkp, dvp, skp, svp, dks, dvs, sks, svs = [], [], [], [], [], [], [], []
    for l in range(DEPTH):
        lam_init = lambda_init(l)
        f32 = jnp.float32
        lam = (jnp.exp(jnp.sum(lambda_q1[l].astype(f32) * lambda_k1[l].astype(f32)))
               - jnp.exp(jnp.sum(lambda_q2[l].astype(f32) * lambda_k2[l].astype(f32))) + lam_init)
        sh1p, sc1p, ga1p, sh2p, sc2p, ga2p = modulate(c_prompt, w_ada[l], b_ada[l])
        sh1s, sc1s, ga1s, sh2s, sc2s, ga2s = modulate(c_sample, w_ada[l], b_ada[l])
        qd, kd, vd, qs, ks_, vs = split_proj(pre(hp, g_pre_mix[l], sh1p, sc1p) @ w_in[l])
        mix_p = prompt_mix(qd, kd, vd, qs, ks_, vs, lam, diff_subln_g[l], lam_init).astype(hp.dtype)
        hp = post_add(hp, mix_p @ w_out[l], g_post_mix[l], ga1p)
        hp = post_add(hp, ffn(pre(hp, g_pre_ffn[l], sh2p, sc2p), w_up[l], w_down[l]), g_post_ffn[l], ga2p)
        dkp.append(kd); dvp.append(vd); skp.append(ks_); svp.append(vs)
        qd, kd, vd, qs, ks_, vs = split_proj(pre(hs, g_pre_mix[l], sh1s, sc1s) @ w_in[l])
        mix_s = sample_mix(qd, kd, vd, qs, ks_, vs, cache_diff_k[l], cache_diff_v[l], cache_sb_k[l], cache_sb_v[l],
                           lam, diff_subln_g[l], lam_init).astype(hs.dtype)
        hs = post_add(hs, mix_s @ w_out[l], g_post_mix[l], ga1s)
        hs = post_add(hs, ffn(pre(hs, g_pre_ffn[l], sh2s, sc2s), w_up[l], w_down[l]), g_post_ffn[l], ga2s)
        dks.append(kd); dvs.append(vd); sks.append(ks_); svs.append(vs)
    diff_k_prompt = jnp.stack(dkp)
    diff_v_prompt = jnp.stack(dvp)
    sb_k_prompt = jnp.stack(skp)
    sb_v_prompt = jnp.stack(svp)
    diff_k_sample = jnp.stack(dks)
    diff_v_sample = jnp.stack(dvs)
    sb_k_sample = jnp.stack(sks)
    sb_v_sample = jnp.stack(svs)
    return (hp, hs, diff_k_prompt, diff_v_prompt, sb_k_prompt, sb_v_prompt,
            diff_k_sample, diff_v_sample, sb_k_sample, sb_v_sample)
```

```python
import math
import numpy as np
from contextlib import ExitStack

import concourse.bass as bass
import concourse.mybir as mybir
from concourse.bass_utils import run_bass_kernel_spmd

F32 = mybir.dt.float32
BF16 = mybir.dt.bfloat16
U8 = mybir.dt.uint8
AF = mybir.ActivationFunctionType
ALU = mybir.AluOpType

D = 1024
SEQ = 2048
NS = 64
PAST = 2048
EPS = 1e-6
LAM_INIT = 0.8 - 0.6 * math.exp(-0.3 * 0)
SLOPES = [2.0 ** (-8.0 * (h + 1) / 4) for h in range(4)]
NEGM = -30000.0
ARENA_BYTES = 206 * 1024


class Tok:
    __slots__ = ("last_w", "readers")

    def __init__(self, hist=None):
        self.last_w = None
        self.readers = dict(hist) if hist else {}


class _Op:
    __slots__ = ("id", "eng", "key", "fn", "deps", "signal", "dma", "sem", "val")


class Sched:
    def __init__(self, nc, n_sp=16, n_pool=8):
        self.nc = nc
        self.ops = []
        self.engs = {"pe": nc.tensor, "act": nc.scalar, "dve": nc.vector, "pool": nc.gpsimd, "sp": nc.sync}
        self.nslots = {"sp": n_sp, "pool": n_pool, "act": 0}
        self.dcount = {"sp": 0, "pool": 0}
        self.slot_prev = {}

    def sem_names(self):
        names = ["pe", "act", "dve", "pool"]
        for q in ("sp", "pool"):
            names += [("dma", q, i) for i in range(self.nslots[q])]
        return names

    def _new(self, eng, fn, dma):
        op = _Op()
        op.id = len(self.ops)
        op.eng = eng
        op.fn = fn
        op.deps = set()
        op.signal = False
        op.dma = dma
        op.sem = None
        op.val = 0
        op.key = eng
        return op

    def _dep(self, op, pid, raw):
        prod = self.ops[pid]
        if not prod.dma and not op.dma and prod.eng == op.eng:
            if not raw or op.eng == "pe":
                return
        op.deps.add(pid)

    def _add(self, op, reads, writes):
        for t in reads:
            if t.last_w is not None:
                self._dep(op, t.last_w, True)
        for t in writes:
            if t.last_w is not None:
                self._dep(op, t.last_w, False)
            for r in t.readers.values():
                self._dep(op, r, False)
        for t in reads:
            t.readers[op.key] = op.id
        for t in writes:
            t.last_w = op.id
            t.readers = {}
        self.ops.append(op)
        return op

    def op(self, eng, fn, reads=(), writes=()):
        return self._add(self._new(eng, fn, False), reads, writes)

    def dma(self, q, fn, reads=(), writes=()):
        op = self._new(q, fn, True)
        slot = self.dcount[q] % self.nslots[q]
        self.dcount[q] += 1
        op.key = op.sem = ("dma", q, slot)
        if op.key in self.slot_prev:
            op.deps.add(self.slot_prev[op.key])
        self.slot_prev[op.key] = op.id
        return self._add(op, reads, writes)

    def emit(self, sems):
        ops = self.ops
        for op in ops:
            for d in op.deps:
                ops[d].signal = True
        counts = {}
        for op in ops:
            if op.dma:
                counts[op.sem] = counts.get(op.sem, 0) + 16
                op.val = counts[op.sem]
                op.signal = True
            elif op.signal:
                op.sem = op.eng
                counts[op.eng] = counts.get(op.eng, 0) + 1
                op.val = counts[op.eng]
        waited = {e: {} for e in self.engs}
        nw = 0
        for op in ops:
            e = self.engs[op.eng]
            w = waited[op.eng]
            need = {}
            for d in op.deps:
                p = ops[d]
                if need.get(p.sem, 0) < p.val:
                    need[p.sem] = p.val
            for key, val in need.items():
                if w.get(key, 0) >= val:
                    continue
                w[key] = val
                e.wait_ge(sems[key], val)
                nw += 1
            ins = op.fn(e)
            if op.signal:
                ins.then_inc(sems[op.sem], 16 if op.dma else 1)
        self.n_wait = nw
        sp = self.engs["sp"]
        for key, val in counts.items():
            if isinstance(key, tuple):
                sp.wait_ge(sems[key], val)


class Buf:
    def __init__(self, arena, off, nbytes, hist, ap):
        self.arena, self.off, self.nbytes, self.hist, self.ap = arena, off, nbytes, hist, ap
        self.toks = {}

    def tok(self, key=None):
        t = self.toks.get(key)
        if t is None:
            t = self.toks[key] = Tok(self.hist)
        return t

    def __getitem__(self, idx):
        return self.ap[idx]


class Arena:
    def __init__(self, sched, ap, total):
        self.S, self.ap, self.total = sched, ap, total
        self.free = [[0, total, {}]]
        self.peak = 0
        self.used = 0

    def alloc(self, nbytes, dt, pat=None, **kw):
        req = nbytes
        nbytes = (nbytes + 31) // 32 * 32
        fl = self.free
        for i in range(len(fl)):
            j, end, hist = i, fl[i][1], dict(fl[i][2])
            while end - fl[i][0] < nbytes and j + 1 < len(fl) and fl[j + 1][0] == end:
                j += 1
                end = fl[j][1]
                self._merge(hist, fl[j][2])
            if end - fl[i][0] >= nbytes:
                start = fl[i][0]
                rest = [[start + nbytes, end, dict(hist)]] if end > start + nbytes else []
                fl[i:j + 1] = rest
                a = self.ap[:, start:start + req].bitcast(dt)
                if pat:
                    a = a.rearrange(pat, **kw)
                self.used += nbytes
                self.peak = max(self.peak, self.used)
                return Buf(self, start, nbytes, hist, a)
        raise MemoryError(f"arena: cannot alloc {nbytes} (used {self.used})")

    def _merge(self, hist, other):
        for k, v in other.items():
            if hist.get(k, -1) < v:
                hist[k] = v

    def release(self, buf):
        hist = dict(buf.hist)
        ops = self.S.ops
        for t in buf.toks.values():
            if t.last_w is not None:
                self._merge(hist, {ops[t.last_w].key: t.last_w})
            self._merge(hist, t.readers)
        self.used -= buf.nbytes
        fl = self.free
        fl.append([buf.off, buf.off + buf.nbytes, hist])
        fl.sort(key=lambda b: b[0])


class Ring:
    def __init__(self, bufs):
        self.bufs, self.i = bufs, 0

    def next(self):
        b = self.bufs[self.i % len(self.bufs)]
        self.i += 1
        return b


def pipeline(items, stages):
    n, k = len(items), len(stages)
    for step in range(n + k - 1):
        for s in range(k - 1, -1, -1):
            i = step - s
            if 0 <= i < n:
                stages[s](items[i])


NCOL = 48 + 32 + 1 + 256 + 51


def make_kbias():
    k = np.arange(128, dtype=np.float64)[:, None]
    n = np.arange(17, dtype=np.float64)[None, :]
    return np.concatenate([SLOPES[h] * (k + 128.0 * (n - 16.0)) for h in (1, 2, 3)], axis=1).astype(np.float32)


def make_consts():
    i = np.arange(128)
    ident = np.eye(128, dtype=np.float32)
    ones = np.ones((128, 128), np.float32)
    negtri = -(i[:, None] >= i[None, :]).astype(np.float32)
    sbmask = np.where(i[:, None] < i[None, :], 0.0, NEGM).astype(np.float32)
    dbs = []
    for h in range(4):
        vis = (i[:, None] // 64) <= (i[None, :] // 64)
        dbs.append(np.where(vis, -SLOPES[h] * np.abs(i[None, :] - i[:, None]), NEGM).astype(np.float32))
    constF = np.concatenate([ident, ones], axis=1)
    nt64 = np.zeros((128, 64), np.float32)
    nt64[:64, :] = negtri[:64, :64]
    dbs2 = [np.concatenate([d_, d_], axis=1) for d_ in dbs]
    akneg = np.zeros((128, 128), np.float32)
    akneg[0:2, :] = -1.0
    constB = np.concatenate([ident, ones, negtri, sbmask, sbmask] + dbs2 + [nt64, akneg], axis=1)
    q = np.arange(512)
    k = np.arange(128)
    al = np.zeros((3, 4 * 512 + 4 * 128), np.float32)
    q = np.concatenate([np.arange(256), np.arange(256)])
    for h in range(4):
        al[0, h * 512:(h + 1) * 512] = -SLOPES[h] * 16 * (q // 16)
        al[1, h * 512:(h + 1) * 512] = -SLOPES[h] * (q % 16)
        al[2, h * 512:(h + 1) * 512] = 1.0
        o = 2048 + h * 128
        al[0, o:o + 128] = 1.0
        al[1, o:o + 128] = 1.0
        al[2, o:o + 128] = SLOPES[h] * k
    return constF, constB, al


def build_program(dbg=False, stop=99):
    nc = bass.Bass("TRN2", target_bir_lowering=False)

    def din(name, shape):
        return nc.dram_tensor(name, shape, F32, kind="ExternalInput").ap()

    def dout(name, shape):
        return nc.dram_tensor(name, shape, F32, kind="ExternalOutput").ap()

    xp = din("xp", [SEQ, D])
    xs = din("xs", [2 * NS, D])
    cT = din("cT", [128, 8, 3])
    cdk = din("cdk", [2, PAST, 512])
    cdv = din("cdv", [2, PAST, 512])
    csk = din("csk", [2, PAST, 512])
    csv = din("csv", [2, PAST, 512])
    w_ada = din("w_ada", [D, 6 * D])
    w_in = din("w_in", [D, 3 * D])
    w_out = din("w_out", [D, D])
    w_up = din("w_up", [D, 4 * D])
    w_down = din("w_down", [4 * D, D])
    cols_d = din("cols", [128, NCOL])
    constF_d = din("constF", [128, 256])
    constB_d = din("constB", [128, 1856])
    alibi_d = din("alibi", [3, 2560])
    yp = dout("yp", [SEQ, D])
    ys = dout("ys", [2 * NS, D])
    kvp = [dout(n, [SEQ, 512]) for n in ("kdp", "vdp", "ksp", "vsp")]
    kvs = [dout(n, [2 * NS, 512]) for n in ("kds", "vds", "kss", "vss")]

    es = ExitStack()
    arena_t = es.enter_context(nc.sbuf_tensor("arena", [128, ARENA_BYTES], U8))
    psum = es.enter_context(nc.psum_tensor("psum", [128, 8, 512], F32))
    S = Sched(nc)
    sems = {n: es.enter_context(nc.semaphore("s_" + "_".join(str(x) for x in (n if isinstance(n, tuple) else (n,)))))
            for n in S.sem_names()}
    A = Arena(S, arena_t, ARENA_BYTES)
    PB = [Tok() for _ in range(8)]

    def finish():
        S.emit(sems)
        es.close()
        return nc, dict(ops=len(S.ops), waits=S.n_wait, peak=A.peak)

    def bv(ap2d):
        return ap2d.rearrange("p (m n) -> p m n", m=2)

    def mm(out, lhsT, rhs, start, stop, reads, writes):
        S.op("pe", lambda e: e.matmul(out, lhsT=lhsT, rhs=rhs, start=start, stop=stop), reads, writes)

    def tr(out, in_, ident, reads, writes):
        S.op("pe", lambda e: e.transpose(out, in_, ident), reads, writes)

    def act(out, in_, func, reads, writes, bias=0.0, scale=1.0, accum=None):
        S.op("act", lambda e: e.activation(out=out, in_=in_, func=func, bias=bias, scale=scale, accum_out=accum),
             reads, writes)

    def ts(eng, out, in0, s1, op0, reads, writes, s2=None, op1=None):
        if op1 is None:
            S.op(eng, lambda e: e.tensor_scalar(out=out, in0=in0, scalar1=s1, scalar2=None, op0=op0), reads, writes)
        else:
            S.op(eng, lambda e: e.tensor_scalar(out=out, in0=in0, scalar1=s1, scalar2=s2, op0=op0, op1=op1),
                 reads, writes)

    def tt(eng, out, in0, in1, op, reads, writes):
        S.op(eng, lambda e: e.tensor_tensor(out=out, in0=in0, in1=in1, op=op), reads, writes)

    def stt(eng, out, in0, scalar, in1, op0, op1, reads, writes):
        S.op(eng, lambda e: e.scalar_tensor_tensor(out=out, in0=in0, scalar=scalar, in1=in1, op0=op0, op1=op1),
             reads, writes)

    def cp(eng, out, in_, reads, writes):
        S.op(eng, lambda e: e.tensor_copy(out=out, in_=in_), reads, writes)

    def dma(q, out, in_, reads, writes):
        S.dma(q, lambda e: e.dma_start(out=out, in_=in_), reads, writes)

    cF = A.alloc(256 * 4, F32)
    cB = A.alloc(1856 * 2, BF16)
    zB = A.alloc(512 * 2, BF16)
    alb = A.alloc(2560 * 2, BF16)
    cols = A.alloc(NCOL * 4, F32)
    modc = A.alloc(6 * 8 * 3 * 4, F32, "p (v c s) -> p v c s", v=6, c=8)
    small = A.alloc(64 * 4, F32)
    stat = A.alloc(64 * 4, F32)
    junk = A.alloc(2048, BF16)
    dgr = Ring([A.alloc(512, F32) for _ in range(2)])
    dma("sp", cF.ap, constF_d, [], [cF.tok()])
    dma("pool", cB.ap, constB_d, [], [cB.tok()])
    S.op("pool", lambda e: e.memset(alb.ap, 0.0), [], [alb.tok()])
    dma("pool", alb.ap[0:3, :], alibi_d, [], [alb.tok()])
    dma("sp", cols.ap, cols_d, [], [cols.tok()])
    S.op("pool", lambda e: e.memset(zB.ap, 0.0), [], [zB.tok()])
    ident_f, ones_f = cF.ap[:, 0:128], cF.ap[:, 128:256]
    ident_b, ones_b = cB.ap[:, 0:128], cB.ap[:, 128:256]
    negtri_b = cB.ap[:, 256:384]
    sbmask2 = cB.ap[:, 384:640].rearrange("p (m n) -> p m n", m=2)
    dbias2 = [cB.ap[:, 640 + 256 * h: 896 + 256 * h].rearrange("p (m n) -> p m n", m=2) for h in range(4)]
    negtri64_b = cB.ap[:, 1664:1728]
    akneg_b = cB.ap[:, 1728:1856]
    aq2 = [alb.ap[:, 512 * h: 512 * (h + 1)].rearrange("p (m n) -> p m n", m=2) for h in range(4)]
    ak = [alb.ap[:, 2048 + 128 * h: 2048 + 128 * (h + 1)] for h in range(4)]
    C_BADA, C_G = 0, 48
    C_SUBLN, C_LAM = 80, 81
    C_KB = 337
    neglam = small.ap[:, 0:1]
    gsub = small.ap[:, 1:2]

    lt = A.alloc(128 * 4, F32)
    tsm = small.tok()
    for i in range(2):
        lq = cols.ap[:, C_LAM + 128 * i: C_LAM + 128 * i + 64]
        lk = cols.ap[:, C_LAM + 128 * i + 64: C_LAM + 128 * i + 128]
        tt("dve", lt.ap[:, 64 * i:64 * i + 64], lq, lk, ALU.mult, [cols.tok()], [lt.tok()])
        S.op("dve", lambda e, i=i: e.reduce_sum(out=small.ap[:, 2 + i:3 + i], in_=lt.ap[:, 64 * i:64 * i + 64],
                                                 axis=mybir.AxisListType.X), [lt.tok()], [tsm])
    act(small.ap[:, 4:6], small.ap[:, 2:4], AF.Exp, [tsm], [tsm])
    tt("dve", small.ap[:, 6:7], small.ap[:, 5:6], small.ap[:, 4:5], ALU.subtract, [tsm], [tsm])
    ts("dve", neglam, small.ap[:, 6:7], -LAM_INIT, ALU.add, [tsm], [tsm])
    ts("dve", gsub, cols.ap[:, C_SUBLN:C_SUBLN + 1], 1.0 - LAM_INIT, ALU.mult, [cols.tok()], [tsm])
    A.release(lt)

    cTf = A.alloc(24 * 4, F32, "p (c s) -> p c s", c=8)
    siluT = A.alloc(24 * 2, BF16, "p (c s) -> p c s", c=8)
    dma("sp", cTf.ap, cT, [], [cTf.tok()])
    act(siluT.ap, cTf.ap, AF.Silu, [cTf.tok()], [siluT.tok()])
    wada = [A.alloc(8 * 1024 * 2, BF16, "p (k n) -> p k n", k=8) for _ in range(2)]
    w_ada_v = w_ada.rearrange("(k p) n -> p k n", p=128)
    psmod = psum[:, 0, 0:144]
    tmv = [modc.tok(v) for v in range(6)]
    POSTG = {1: 0, 4: 2}
    POSTA = {2: 1, 5: 3}

    def mod_load(v, wb, j0=0, j1=8):
        dma("pool", wb.ap[:, :, 0:128 * (j1 - j0)], w_ada_v[:, :, v * 1024 + 128 * j0:v * 1024 + 128 * j1], [], [wb.tok()])

    def mod_block(v, wb, j0=0, j1=8):
        for j in range(j0, j1):
            for kc in range(8):
                mm(psmod[:, (v * 8 + j) * 3:(v * 8 + j) * 3 + 3], wb.ap[:, kc, (j - j0) * 128:(j - j0 + 1) * 128],
                   siluT.ap[:, kc, :], kc == 0, kc == 7, [wb.tok(), siluT.tok()], [PB[0]])
        pv = psmod[:, v * 24:(v + 1) * 24].rearrange("p (c s) -> p c s", c=8)
        for s in range(3):
            md = modc.ap[:, v, j0:j1, s]
            tt("dve", md, pv[:, j0:j1, s], cols.ap[:, C_BADA + v * 8 + j0: C_BADA + v * 8 + j1], ALU.add,
               [PB[0], cols.tok()], [tmv[v]])
            if v in POSTG:
                gi = POSTG[v]
                stt("dve", md, md, 1.0, cols.ap[:, C_G + 8 * gi + j0: C_G + 8 * gi + j1],
                    ALU.add, ALU.mult, [tmv[v], cols.tok()], [tmv[v]])
            if v in POSTA:
                gi = POSTA[v]
                tt("dve", md, md, cols.ap[:, C_G + 8 * gi + j0: C_G + 8 * gi + j1],
                   ALU.mult, [tmv[v], cols.tok()], [tmv[v]])

    win = A.alloc(8 * 3072 * 2, BF16, "p (k n) -> p k n", k=8)
    w_in_v = w_in.rearrange("(k p) n -> p k n", p=128)
    for v in range(2):
        mod_load(v, wada[v % 2])
        if v == 1:
            for blk in range(3):
                dma("pool", win.ap[:, :, blk * 1024:(blk + 1) * 1024], w_in_v[:, :, blk * 1024:(blk + 1) * 1024],
                    [], [win.tok(blk)])
        mod_block(v, wada[v % 2])
    for b in wada:
        A.release(b)

    if stop <= 0:
        return finish()
    tiles = [dict(s=0, P=128, col=128 * i, row=128 * i, x=xp, y=yp, kv=kvp) for i in range(16)]
    tiles += [dict(s=1 + b, P=64, col=2048 + 64 * b, row=64 * b, x=xs, y=ys, kv=kvs) for b in range(2)]
    ytok = [Tok() for _ in tiles]

    qTs = A.alloc(8 * 128 * 2, BF16, "p (c t) -> p c t", c=8)
    kTs = A.alloc(8 * 128 * 2, BF16, "p (c t) -> p c t", c=8)
    vBs = A.alloc(2 * 1024 * 2, BF16, "p (i n) -> p i n", i=2)
    qT = A.alloc(8 * 2048 * 2, BF16, "p (c t) -> p c t", c=8)
    kT = A.alloc(8 * 2048 * 2, BF16, "p (c t) -> p c t", c=8)
    vB = A.alloc(16 * 1024 * 2, BF16, "p (i n) -> p i n", i=16)

    stat_i = [0]

    def rms_rstd(parts, P, junk, n_feat, reads):
        k = stat_i[0] % 8
        stat_i[0] += 1
        t = stat.tok(k)
        base = stat.ap[:, 8 * k: 8 * k + 8]
        for i, pa in enumerate(parts):
            act(junk.ap[:P, 0:pa.shape[1]], pa, AF.Square, reads, [junk.tok(), t], accum=base[:P, i:i + 1])
        if len(parts) == 2:
            tt("dve", base[:P, 0:1], base[:P, 0:1], base[:P, 1:2], ALU.add, [t], [t])
        act(base[:P, 2:3], base[:P, 0:1], AF.Ln, [t], [t], bias=EPS, scale=1.0 / n_feat)
        act(base[:P, 3:4], base[:P, 2:3], AF.Exp, [t], [t], scale=-0.5)
        return base[:P, 3:4], t

    def prenorm_group(gt, load_x, hT, vsh, vG, xring, xsring, junk, keep=None):
        off = 0
        for td in gt:
            P = td["P"]
            xt = xring.next()
            load_x(td, xt)
            rstd, tst = rms_rstd([xt.ap[:P, :]], P, junk, D, [xt.tok()])
            xsb = xsring.next()
            ts("dve", xsb.ap[:P, :], xt.ap[:P, :], rstd, ALU.mult, [xt.tok(), tst], [xsb.tok()])
            for c in range(8):
                pv = psum[:, c // 2, (c % 2) * 256:(c % 2) * 256 + 256]
                tr(pv[:, off:off + P], xsb.ap[:P, c * 128:(c + 1) * 128], ident_f[:P, :P],
                   [xsb.tok(), cF.tok()], [PB[c // 2]])
            if keep is not None:
                keep.append(xt)
            off += P
        for c in range(8):
            pv = psum[:, c // 2, (c % 2) * 256:(c % 2) * 256 + 256]
            o = 0
            for td in gt:
                P, s = td["P"], td["s"]
                act(hT.ap[:, c, o:o + P], pv[:, o:o + P], AF.Identity, [PB[c // 2], tmv[vsh], tmv[vG]], [hT.tok()],
                    bias=modc.ap[:, vsh, c, s:s + 1], scale=modc.ap[:, vG, c, s:s + 1])
                o += P
        return off

    xring = Ring([A.alloc(4096, F32) for _ in range(3)])
    xsring = Ring([A.alloc(4096, F32) for _ in range(1)])
    hring = Ring([A.alloc(8 * 256 * 2, BF16, "p (c t) -> p c t", c=8) for _ in range(2)])
    stg = Ring([A.alloc(2048, F32) for _ in range(4)])
    groups = [tiles[2 * g:2 * g + 2] for g in range(8)] + [tiles[16:18]]
    pring = Ring([4, 5, 6, 7])
    evq = [0]
    FM = [(0, 0, 0.125), (0, 1, 0.125), (0, 2, 0.125), (0, 3, 0.125), (1, 0, 1.0), (1, 1, 1.0), (1, 2, 1.0), (1, 3, 1.0),
          (0, 4, 0.125), (0, 5, 0.125), (0, 6, 0.125), (0, 7, 0.125), (1, 4, 1.0), (1, 5, 1.0), (1, 6, 1.0), (1, 7, 1.0)]
    FMCOL = [0, 128, 256, 384, 512, 640, 768, 896, 1536, 1664, 1792, 1920, 2048, 2176, 2304, 2432]

    def load_x0(td, xt):
        dma("sp", xt.ap[:td["P"], :], td["x"][td["row"]:td["row"] + td["P"], :], [], [xt.tok()])

    def proj_fm(g, gt, hT, N):
        c0 = gt[0]["col"]
        for oc in range(16):
            which, idx, scl = FM[oc]
            wc = FMCOL[oc]
            b = pring.next()
            for kc in range(8):
                mm(psum[:, b, 0:N], win.ap[:, kc, wc:wc + 128], hT.ap[:, kc, 0:N], kc == 0, kc == 7,
                   [win.tok(wc // 1024), hT.tok()], [PB[b]])
            if g < 8:
                dst, dc0 = (qT if which == 0 else kT), c0
            else:
                dst, dc0 = (qTs if which == 0 else kTs), c0 - 2048
            eng = "act" if evq[0] % 2 == 0 else "dve"
            evq[0] += 1
            if eng == "act":
                act(dst.ap[:, idx, dc0:dc0 + N], psum[:, b, 0:N], AF.Identity, [PB[b]], [dst.tok((idx, g))], scale=scl)
            else:
                ts("dve", dst.ap[:, idx, dc0:dc0 + N], psum[:, b, 0:N], scl, ALU.mult, [PB[b]], [dst.tok((idx, g))])

    def proj_tm(g, gt, hT):
        o = 0
        for td in gt:
            P = td["P"]
            ti = tiles.index(td)
            for bi, wc in enumerate((512, 1024, 2048, 2560)):
                b = pring.next()
                for kc in range(8):
                    mm(psum[:P, b, :], hT.ap[:, kc, o:o + P], win.ap[:, kc, wc:wc + 512], kc == 0, kc == 7,
                       [win.tok(wc // 1024), hT.tok()], [PB[b]])
                sb_ = stg.next()
                eng = "act" if evq[0] % 2 == 0 else "dve"
                evq[0] += 1
                if eng == "act":
                    act(sb_.ap[:P, :], psum[:P, b, :], AF.Identity, [PB[b]], [sb_.tok()])
                else:
                    cp("dve", sb_.ap[:P, :], psum[:P, b, :], [PB[b]], [sb_.tok()])
                dma("sp", td["kv"][bi][td["row"]:td["row"] + P, :], sb_.ap[:P, :], [sb_.tok()], [Tok()])
                if bi in (1, 3):
                    h = 0 if bi == 1 else 1
                    vdst, vti = (vB, ti) if ti < 16 else (vBs, ti - 16)
                    cp("pool" if P == 128 else "dve", vdst.ap[:P, vti, 512 * h:512 * h + 512], sb_.ap[:P, :],
                       [sb_.tok()], [vdst.tok((vti, h))])
            o += P

    hT = hring.next()
    cur = (hT, prenorm_group(groups[0], load_x0, hT, 0, 1, xring, xsring, junk))
    for g, gt in enumerate(groups):
        hT, N = cur
        proj_fm(g, gt, hT, N)
        if g + 1 < len(groups):
            hTn = hring.next()
            cur = (hTn, prenorm_group(groups[g + 1], load_x0, hTn, 0, 1, xring, xsring, junk))
        proj_tm(g, gt, hT)
    for r in (xring, xsring, hring, stg):
        for b in r.bufs:
            A.release(b)
    A.release(win)

    if stop <= 1:
        return finish()
    wout = A.alloc(8 * 1024 * 2, BF16, "p (k n) -> p k n", k=8)
    dma("pool", wout.ap, w_out.rearrange("(k p) n -> p k n", p=128), [], [wout.tok()])
    gab = A.alloc(4096, F32)
    mixT = A.alloc(8 * 512 * 2, BF16, "p (c t) -> p c t", c=8)
    ering = Ring([A.alloc(1024, BF16) for _ in range(4)])
    fring = Ring([A.alloc(2048, F32) for _ in range(4)])
    for bfr in ering.bufs + fring.bufs:
        S.op("pool", lambda e, bfr=bfr: e.memset(bfr.ap, 0.0), [], [bfr.tok()])
    qzr = Ring([A.alloc(1024, BF16) for _ in range(3)])
    for bfr in qzr.bufs:
        S.op("pool", lambda e, bfr=bfr: e.memset(bfr.ap, 0.0), [], [bfr.tok()])
    ftmps = [[A.alloc(2048, F32) for _ in range(3)] for _ in range(2)]
    ftmp = ftmps[0] + ftmps[1]
    unit_par = [0]

    def make_qz(qsrc, idx, qc0, N, rq):
        qz = qzr.next()
        qzv = bv(qz.ap)
        cp("dve", qzv[0:64, 0, 0:N], qsrc.ap[0:64, idx, qc0:qc0 + N], rq, [qz.tok()])
        cp("dve", qzv[64:128, 1, 0:N], qsrc.ap[64:128, idx, qc0:qc0 + N], rq, [qz.tok()])
        return qz, qzv

    Rb = [A.alloc(2048, F32) for _ in range(2)]
    xring = Ring([A.alloc(4096, F32) for _ in range(2)])
    tring = Ring([A.alloc(4096, F32) for _ in range(2)])

    def make_bcast(v, s, P, dst):
        for c in range(8):
            dg = dgr.next()
            ts("dve", dg.ap, ident_f, modc.ap[:, v, c, s:s + 1], ALU.mult, [cF.tok(), tmv[v]], [dg.tok()])
            b = 6 + c // 4
            mm(psum[:P, b, (c % 4) * 128:(c % 4) * 128 + 128], ones_f[:, :P], dg.ap, True, True,
               [cF.tok(), dg.tok()], [PB[b]])
        for hb in range(2):
            cp("dve", dst.ap[:P, 512 * hb:512 * hb + 512], psum[:P, 6 + hb, :], [PB[6 + hb]], [dst.tok()])

    def post_add(td, ti, ybanks, gab_, xt, out_dram, xtok_reads, junk_):
        P = td["P"]
        yv = psum[:P, ybanks[0]:ybanks[0] + 2, :]
        rstd, tst = rms_rstd([yv[:, 0, :], yv[:, 1, :]], P, junk_, D, [PB[ybanks[0]], PB[ybanks[1]]])
        t = tring.next()
        for hb in range(2):
            stt("dve", t.ap[:P, 512 * hb:512 * hb + 512], yv[:, hb, :], rstd, gab_.ap[:P, 512 * hb:512 * hb + 512],
                ALU.mult, ALU.mult, [PB[ybanks[hb]], tst, gab_.tok()], [t.tok()])
        tt("pool" if P == 128 else "dve", t.ap[:P, :], t.ap[:P, :], xt.ap[:P, :], ALU.add, [t.tok(), xt.tok()], [t.tok()])
        dma("sp", out_dram[td["row"]:td["row"] + P, :], t.ap[:P, :], [t.tok()], [ytok[ti]])

    from collections import deque
    pending = deque()

    def stream(items, nstages, stage_fn, pre_stage=None):
        n = len(items)
        for step in range(n + nstages - 1):
            if pre_stage is not None and 0 <= step - pre_stage < n:
                stage_fn(-pre_stage, items[step - pre_stage])
            for s_ in range(nstages - 1, -1, -1):
                i = step - s_
                if 0 <= i < n:
                    stage_fn(s_, items[i])
            if pending:
                pending.popleft()[1]()

    def flush_pending(par=None):
        if par is None:
            while pending:
                pending.popleft()[1]()
            return
        last = -1
        for i_, (p_, _) in enumerate(pending):
            if p_ == par:
                last = i_
        for _ in range(last + 1):
            pending.popleft()[1]()

    def diff_stream(N, qsrc, groups_):
        sring = Ring([4, 5])
        units = []
        for gd in groups_:
            for hd in gd["heads"]:
                par = unit_par[0] % 2
                unit_par[0] += 1
                units.append(dict(hd=hd, par=par, OB=2 * par, DB=2 * par + 1, qz=None,
                                  kts=gd["kts"], qc0=gd["qc0"], moff=gd["moff"]))
        items = [(u, kt) for u in units for kt in u["kts"]]

        def mkq(u):
            u["qz"] = make_qz(qsrc, u["hd"], u["qc0"], N, [qsrc.tok(k) for k in u["kts"][0]["qtoks"](u["hd"])])
        mkq(units[0])
        state = {}

        def finalizers(u):
            hd, par = u["hd"], u["par"]
            Ov, Dv = bv(psum[:, u["OB"], :]), bv(psum[:, u["DB"], :])
            ta, tb, tc = ftmps[par]
            tav, tbv = bv(ta.ap), bv(tb.ap)
            mix_dst, mix_tok = mixT.ap[:, hd, u["moff"]:u["moff"] + N], mixT.tok(u["moff"])
            fo = 256 * par

            def f1():
                act(tav[:, :, :N], Dv[:, :, :N], AF.Ln, [PB[u["DB"]]], [ta.tok()])
                act(tav[:, :, :N], tav[:, :, :N], AF.Exp, [ta.tok()], [ta.tok()], scale=-1.0)

            def f2():
                tt("dve", tbv[:, :, :N], Ov[:, :, :N], tav[:, :, :N], ALU.mult, [PB[u["OB"]], ta.tok()], [tb.tok()])
                stt("dve", tc.ap[:, :N], tbv[:, 1, :N], neglam, tbv[:, 0, :N], ALU.mult, ALU.add,
                    [tb.tok(), tsm], [tc.tok()])

            thl = ta.ap[:, 256:512].bitcast(BF16)

            def f3():
                tt("dve", ta.ap[:, :N], tc.ap[:, :N], tc.ap[:, :N], ALU.mult, [tc.tok()], [ta.tok()])
                cp("dve", thl[:, 0:N], ta.ap[:, :N], [ta.tok()], [ta.tok()])
                tt("dve", thl[:, 256:256 + N], ta.ap[:, :N], thl[:, 0:N], ALU.subtract, [ta.tok()], [ta.tok()])

            fbs = {}

            def f4():
                FB = fbs["b"] = sring.next()
                mm(psum[:, FB, fo:fo + N], ones_b, thl[:, 0:N], True, False, [cB.tok(), ta.tok()], [PB[FB]])
                mm(psum[:, FB, fo:fo + N], ones_b, thl[:, 256:256 + N], False, True, [cB.tok(), ta.tok()], [PB[FB]])

            def f5():
                FB = fbs["b"]
                act(tb.ap[:, :N], psum[:, FB, fo:fo + N], AF.Ln, [PB[FB]], [tb.tok()], bias=EPS, scale=1.0 / 128)
                act(tb.ap[:, :N], tb.ap[:, :N], AF.Exp, [tb.tok()], [tb.tok()], scale=-0.5)

            def f6():
                stt("dve", mix_dst, tc.ap[:, :N], gsub, tb.ap[:, :N], ALU.mult, ALU.mult,
                    [tb.tok(), tc.tok(), tsm], [mix_tok])
            return [f1, f2, f3, f4, f5, f6]

        def stage(s_, it):
            u, kt = it
            hd = u["hd"]
            nk, c0, nd = kt["nk"], kt["col0"], kt["nd"]
            lin = c0 + nd < N
            if s_ == 0:
                if kt is u["kts"][0]:
                    ui = units.index(u)
                    if ui + 1 < len(units):
                        mkq(units[ui + 1])
                qz, qzv = u["qz"]
                b = sring.next()
                Sv = bv(psum[:, b, :])
                if hd == 0:
                    mm(Sv[:nk, :, c0:N], kt["kap"](hd), qzv[:, :, c0:N], True, False, [qz.tok()] + kt["ktoks"](hd), [PB[b]])
                    if nd:
                        mm(Sv[:nk, :, c0:c0 + nd], ident_b[:, :nk], dbias2[hd][:, :, :nd], False, not lin, [cB.tok()], [PB[b]])
                    if lin:
                        mm(Sv[:nk, :, c0 + nd:N], ak[hd][:, :nk], aq2[hd][:, :, c0 + nd:N], False, True, [alb.tok()], [PB[b]])
                else:
                    mm(Sv[:nk, :, c0:N], kt["kap"](hd), qzv[:, :, c0:N], True, not nd, [qz.tok()] + kt["ktoks"](hd), [PB[b]])
                    if nd:
                        mm(Sv[:nk, :, c0:c0 + nd], ident_b[:, :nk], dbias2[hd][:, :, :nd], False, False, [cB.tok()], [PB[b]])
                        mm(Sv[:nk, :, c0:c0 + nd], akneg_b[:, :nk], aq2[hd][:, :, c0:c0 + nd], False, True,
                           [cB.tok(), alb.tok()], [PB[b]])
                state[(id(u), id(kt))] = b
            elif s_ == 1:
                b = state[(id(u), id(kt))]
                Sv = bv(psum[:, b, :])
                E = ering.next()
                Ev = bv(E.ap)
                if nd:
                    act(Ev[:nk, :, c0:c0 + nd], Sv[:nk, :, c0:c0 + nd], AF.Exp, [PB[b]], [E.tok()])
                if lin:
                    if hd == 0:
                        act(Ev[:nk, :, c0 + nd:N], Sv[:nk, :, c0 + nd:N], AF.Exp, [PB[b]], [E.tok()],
                            bias=float(kt["const"] * SLOPES[hd]))
                    else:
                        kc_ = C_KB + (hd - 1) * 17 + (kt["const"] // 128 + 16)
                        act(Ev[:nk, :, c0 + nd:N], Sv[:nk, :, c0 + nd:N], AF.Exp, [PB[b], cols.tok()], [E.tok()],
                            bias=cols.ap[:nk, kc_:kc_ + 1])
                state[(id(u), id(kt))] = E
            else:
                E = state.pop((id(u), id(kt)))
                Ev = bv(E.ap)
                Ov, Dv = bv(psum[:, u["OB"], :]), bv(psum[:, u["DB"], :])
                first, last = kt is u["kts"][0], kt is u["kts"][-1]
                if first:
                    flush_pending(u["par"])
                mm(Ov[:, :, c0:N], kt["v"](hd), Ev[:nk, :, c0:N], first, last, [E.tok()] + kt["vtok"](0), [PB[u["OB"]]])
                mm(Dv[:, :, c0:N], ones_b[:nk, :], Ev[:nk, :, c0:N], first, last, [E.tok(), cB.tok()], [PB[u["DB"]]])
                if last:
                    pending.extend((u["par"], f_) for f_ in finalizers(u))

        if min(len(gd["kts"]) for gd in groups_) >= 2:
            stream(items, 3, stage)
        else:
            for u in units:
                stream([(u, kt) for kt in u["kts"]], 3, stage)
                flush_pending()

    def sb_stream(N, qsrc, groups_):
        minlen = min(len(gd["kts"]) for gd in groups_)
        if minlen >= 5:
            obanks, aring = [0, 1], Ring([2, 3, 4, 7])
        else:
            assert minlen >= 2
            obanks, aring = [0, 1, 7], Ring([2, 3, 4])
        cring = Ring([5, 6])
        units = []
        for gd in groups_:
            for j in gd["heads"]:
                par = unit_par[0] % len(obanks)
                unit_par[0] += 1
                units.append(dict(j=j, par=par, OB=obanks[par], R=(Rb + [ftmps[1][0]])[par], qz=None,
                                  kts=gd["kts"], qc0=gd["qc0"], moff=gd["moff"]))
        items = [(u, kt) for u in units for kt in u["kts"]]

        def mkq(u):
            u["qz"] = make_qz(qsrc, 4 + u["j"], u["qc0"], N, [qsrc.tok(k) for k in u["kts"][0]["qtoks"](4 + u["j"])])
        mkq(units[0])
        state = {}

        def stage(s_, it):
            u, kt = it
            j, R, OB = u["j"], u["R"], u["OB"]
            Rv = bv(R.ap)
            Ov = bv(psum[0:64, OB, :])
            nk, c0, nd = kt["nk"], kt["col0"], kt["nd"]
            key = (id(u), id(kt))
            if s_ == 0:
                if kt is u["kts"][0]:
                    ui = units.index(u)
                    if ui + 1 < len(units):
                        mkq(units[ui + 1])
                    flush_pending(u["par"])
                    S.op("pool", lambda e: e.memset(R.ap, 0.0), [], [R.tok()])
                    for hh in (0, 1):
                        mm(Ov[:, hh, 0:N], zB.ap[:, 0:64], zB.ap[:, 0:N], True, False, [zB.tok()], [PB[OB]])
                qz, qzv = u["qz"]
                b = aring.next()
                Av = bv(psum[:, b, :])
                mm(Av[:nk, :, c0:N], kt["kap"](4 + j), qzv[:, :, c0:N], True, False,
                   [qz.tok()] + kt["ktoks"](4 + j), [PB[b]])
                if nd:
                    mm(Av[:nk, :, c0:c0 + nd], ident_b[:, :nk], sbmask2[:, :, :nd], False, False, [cB.tok()], [PB[b]])
                state[key] = dict(a=b)
                return
            st = state[key]
            Av = bv(psum[:, st["a"], :])
            if s_ == -1:
                ef = fring.next()
                act(bv(ef.ap)[:nk, :, c0:N], Av[:nk, :, c0:N], AF.Exp, [PB[st["a"]]], [ef.tok()])
                st["ef"] = ef
            elif s_ == 1:
                ef = st["ef"]
                sp = ering.next()
                act(bv(sp.ap)[:nk, :, c0:N], bv(ef.ap)[:nk, :, c0:N], AF.Ln, [ef.tok()], [sp.tok()], bias=1.0)
                st["sp"] = sp
            elif s_ == 2:
                cb = cring.next()
                Cv = bv(psum[:, cb, :])
                spv = bv(st["sp"].ap)
                if nk == 128:
                    mm(Av[:, :, c0:N], negtri_b, spv[:, :, c0:N], False, True, [cB.tok(), st["sp"].tok()], [PB[st["a"]]])
                else:
                    mm(Av[:nk, :, c0:N], negtri64_b, spv[:, :, c0:N], False, True, [cB.tok(), st["sp"].tok()], [PB[st["a"]]])
                mm(Cv[:, :, c0:N], ones_b[:nk, :], spv[:nk, :, c0:N], True, True, [cB.tok(), st["sp"].tok()], [PB[cb]])
                st["c"] = cb
            elif s_ == 3:
                Cv = bv(psum[:, st["c"], :])
                bs = fring.next()
                tt("dve", bv(bs.ap)[:nk, :, c0:N], Av[:nk, :, c0:N], Rv[:nk, :, c0:N], ALU.subtract,
                   [PB[st["a"]], R.tok()], [bs.tok()])
                tt("dve", Rv[:, :, c0:N], Cv[:, :, c0:N], Rv[:, :, c0:N], ALU.add, [PB[st["c"]], R.tok()], [R.tok()])
                st["bs"] = bs
            elif s_ == 4:
                w = ering.next()
                act(bv(w.ap)[:nk, :, c0:N], bv(st["bs"].ap)[:nk, :, c0:N], AF.Exp, [st["bs"].tok()], [w.tok()])
                st["w"] = w
            else:
                state.pop(key)
                last = kt is u["kts"][-1]
                wv = bv(st["w"].ap)
                for hh in (0, 1):
                    mm(Ov[:, hh, c0:N], kt["v"](4 + j, hh), wv[:nk, hh, c0:N], False, last,
                       [st["w"].tok()] + kt["vtok"](1), [PB[OB]])
                if last:
                    mix_dst, mix_tok = mixT.ap[:, 4 + j, u["moff"]:u["moff"] + N], mixT.tok(u["moff"])

                    def fin():
                        cp("dve", mix_dst[0:64, :], Ov[:, 0, 0:N], [PB[OB]], [mix_tok])
                        cp("dve", mix_dst[64:128, :], Ov[:, 1, 0:N], [PB[OB]], [mix_tok])
                    pending.append((u["par"], fin))

        stream(items, 6, stage, pre_stage=1)

    def run_units(N, qsrc, qc0, kts, specs):
        run_groups(N, qsrc, [dict(qc0=qc0, kts=kts, specs=specs, moff=mix_off[0])])

    def run_groups(N, qsrc, gl, mid_fn=None):
        dg = [dict(qc0=g_["qc0"], kts=g_["kts"], moff=g_["moff"], heads=[i for k, i in g_["specs"] if k == "d"]) for g_ in gl]
        sg = [dict(qc0=g_["qc0"], kts=g_["kts"][::-1], moff=g_["moff"], heads=[i for k, i in g_["specs"] if k == "s"])
              for g_ in gl]
        dg = [x for x in dg if x["heads"]]
        sg = [x for x in sg if x["heads"]]
        if dg:
            diff_stream(N, qsrc, dg)
            flush_pending()
        if mid_fn is not None:
            mid_fn()
        if sg:
            sb_stream(N, qsrc, sg)
        flush_pending()

    mix_off = [0]

    def wout_closures(gt, gab_, junk_, moff):
        cl = []
        o = moff
        for td in gt:
            def f(td=td, o=o):
                P = td["P"]
                ti = tiles.index(td)
                xt = xring.next()
                dma("sp", xt.ap[:P, :], td["x"][td["row"]:td["row"] + P, :], [], [xt.tok()])
                for half in range(2):
                    for kc in range(8):
                        mm(psum[:P, 6 + half, :], mixT.ap[:, kc, o:o + P], wout.ap[:, kc, half * 512:half * 512 + 512],
                           kc == 0, kc == 7, [mixT.tok(moff), wout.tok()], [PB[6 + half]])
                post_add(td, ti, (6, 7), gab_, xt, td["y"], None, junk_)
            cl.append(f)
            o += td["P"]
        return cl

    def wout_post(gt, N, gab_, junk_):
        for f in wout_closures(gt, gab_, junk_, mix_off[0]):
            f()

    wlate = A.alloc(8 * 512 * 2, BF16, "p (k n) -> p k n", k=8)
    late = [(2, 0, 4), (2, 4, 8), (3, 0, 4), (3, 4, 8), (4, 0, 4), (4, 4, 8), (5, 0, 4), (5, 4, 8)]
    mod_load(late[0][0], wlate, late[0][1], late[0][2])

    def prompt_ktiles(g):
        kts = []
        for J in range(2 * g + 2):
            r = J - 2 * g
            c0 = 128 * r if r >= 0 else 0
            kts.append(dict(
                nk=128, col0=c0, nd=128 if r >= 0 else 0, const=-(256 * g - 128 * J),
                kap=lambda idx, J=J: kT.ap[:, idx, 128 * J:128 * J + 128],
                ktoks=lambda idx, J=J: [kT.tok((idx, J // 2))],
                qtoks=lambda idx, g=g: [(idx, g)],
                v=(lambda a, hh=None, J=J: vB.ap[:, J, 128 * a:128 * a + 128] if hh is None
                   else vB.ap[:, J, 512 + 64 * (2 * (a - 4) + hh): 512 + 64 * (2 * (a - 4) + hh) + 64]),
                vtok=lambda h, J=J: [vB.tok((J, h))]))
        return kts

    late_k = [0]

    def late_step():
        lv, l0, l1 = late[late_k[0]]
        mod_block(lv, wlate, l0, l1)
        late_k[0] += 1
        if late_k[0] < len(late):
            lv, l0, l1 = late[late_k[0]]
            mod_load(lv, wlate, l0, l1)
        else:
            A.release(wlate)

    allspec = [("d", h) for h in range(4)] + [("s", j) for j in range(4)]
    for p in range(4):
        gl = [dict(qc0=256 * g, kts=prompt_ktiles(g), specs=allspec, moff=256 * (g % 2)) for g in (2 * p, 2 * p + 1)]
        run_groups(256, qT, gl, mid_fn=late_step)
        late_step()
        if p == 0:
            make_bcast(2, 0, 128, gab)
        for g in (2 * p, 2 * p + 1):
            cl = wout_closures(tiles[2 * g:2 * g + 2], gab, junk, 256 * (g % 2))
            if p < 3:
                pending.extend((None, f_) for f_ in cl)
            else:
                for f_ in cl:
                    f_()

    flush_pending()
    mix_off[0] = 0
    if stop <= 2:
        return finish()
    for bfr in (qT, kT, vB):
        A.release(bfr)
    wup0 = A.alloc(8 * 1024 * 2, BF16, "p (k n) -> p k n", k=8)
    ksts = [A.alloc(16 * 512 * 2, BF16, "p (i n) -> p i n", i=16) for _ in range(2)]
    vCs = [A.alloc(16 * 512 * 2, BF16, "p (i n) -> p i n", i=16) for _ in range(2)]
    kTc = A.alloc(4 * 2048 * 2, BF16, "p (c t) -> p c t", c=4)
    seqk = [(0, "d", cdk, cdv), (0, "s", csk, csv), (1, "d", cdk, cdv), (1, "s", csk, csv)]
    w_up_v = w_up.rearrange("(k p) n -> p k n", p=128)

    def cache_load(n):
        b_, _, ck_, cv_ = seqk[n]
        dma("pool", ksts[n % 2].ap, ck_[b_].rearrange("(i p) n -> p i n", p=128), [], [ksts[n % 2].tok()])
        dma("pool", vCs[n % 2].ap, cv_[b_].rearrange("(i p) n -> p i n", p=128), [], [vCs[n % 2].tok()])
    cache_load(0)
    cache_load(1)
    for b in range(2):
        td = tiles[16 + b]
        qc0 = 64 * b
        make_bcast(2, 1 + b, 64, gab)
        for kind, ck, cv in (("d", cdk, cdv), ("s", csk, csv)):
            n_ = 2 * b + (0 if kind == "d" else 1)
            kst, vC = ksts[n_ % 2], vCs[n_ % 2]
            for c in range(4):
                for half in range(2):
                    bnk = 4 + (2 * c + half) % 2
                    pb = psum[:, bnk, :].bitcast(BF16)
                    for i in range(8):
                        tr(pb[:, 128 * i:128 * i + 128], kst.ap[:, 8 * half + i, 128 * c:128 * c + 128], ident_b,
                           [kst.tok(), cB.tok()], [PB[bnk]])
                    if half:
                        cp("dve", kTc.ap[:, c, 1024:2048], pb, [PB[bnk]], [kTc.tok()])
                    else:
                        act(kTc.ap[:, c, 0:1024], pb, AF.Identity, [PB[bnk]], [kTc.tok()])
            if stop == 2.1:
                return finish()
            off = 0 if kind == "d" else 4
            kts = []
            for J in range(16):
                kts.append(dict(
                    nk=128, col0=0, nd=0, const=-(2048 - 128 * J),
                    kap=lambda idx, J=J, off=off: kTc.ap[:, idx - off, 128 * J:128 * J + 128],
                    ktoks=lambda idx: [kTc.tok()],
                    qtoks=lambda idx: [(idx, 8)],
                    v=(lambda a, hh=None, J=J: vC.ap[:, J, 128 * a:128 * a + 128] if hh is None
                       else vC.ap[:, J, 64 * (2 * (a - 4) + hh): 64 * (2 * (a - 4) + hh) + 64]),
                    vtok=lambda h: [vC.tok()]))
            kts.append(dict(
                nk=64, col0=0, nd=64, const=0,
                kap=lambda idx, qc0=qc0: kTs.ap[:, idx, qc0:qc0 + 64],
                ktoks=lambda idx: [kTs.tok((idx, 8))],
                qtoks=lambda idx: [(idx, 8)],
                v=(lambda a, hh=None, b=b: vBs.ap[:64, b, 128 * a:128 * a + 128] if hh is None
                   else vBs.ap[:64, b, 512 + 64 * (2 * (a - 4) + hh): 512 + 64 * (2 * (a - 4) + hh) + 64]),
                vtok=lambda h, b=b: [vBs.tok((b, h))]))
            if kind == "d":
                run_units(64, qTs, qc0, kts, [("d", h) for h in range(4)])
            else:
                run_units(64, qTs, qc0, kts, [("s", j) for j in range(4)])
            if n_ + 2 < 4:
                cache_load(n_ + 2)
            if n_ == 1:
                dma("pool", wup0.ap, w_up_v[:, :, 0:1024], [], [wup0.tok()])
        if stop == 2.3:
            return finish()
        wout_post([td], 64, gab, junk)

    for bfr in ksts + vCs + [kTc, qTs, kTs, vBs, wout, gab, mixT] + ering.bufs + fring.bufs + ftmp + Rb + xring.bufs + tring.bufs + qzr.bufs:
        A.release(bfr)

    if stop <= 3:
        return finish()
    wup = A.alloc(8 * 3072 * 2, BF16, "p (k n) -> p k n", k=8)
    wdn = A.alloc(32 * 1024 * 2, BF16, "p (k n) -> p k n", k=32)
    w_dn_v = w_down.rearrange("(k p) n -> p k n", p=128)
    for blk in range(1, 4):
        dma("pool", wup.ap[:, :, (blk - 1) * 1024:blk * 1024], w_up_v[:, :, blk * 1024:(blk + 1) * 1024], [], [wup.tok(blk)])
    for blk in range(4):
        dma("pool", wdn.ap[:, 8 * blk:8 * blk + 8, :], w_dn_v[:, 8 * blk:8 * blk + 8, :], [], [wdn.tok(blk)])
    gab2 = A.alloc(4096, F32)
    gab = gab2
    x1ring = Ring([A.alloc(4096, F32) for _ in range(4)])
    xsring = Ring([A.alloc(4096, F32) for _ in range(1)])
    tring = Ring([A.alloc(4096, F32) for _ in range(2)])
    h2ring = Ring([A.alloc(8 * 256 * 2, BF16, "p (c t) -> p c t", c=8) for _ in range(2)])
    hid = A.alloc(32 * 256 * 2, BF16, "p (c t) -> p c t", c=32)
    rl = Ring([A.alloc(2048, F32) for _ in range(2)])

    def load_x1(td, xt):
        ti = tiles.index(td)
        dma("sp", xt.ap[:td["P"], :], td["y"][td["row"]:td["row"] + td["P"], :], [ytok[ti]], [xt.tok()])

    make_bcast(5, 0, 128, gab2)

    def ffn_up(h2, N):
        ub = Ring([4, 5])
        for fp in range(16):
            b = ub.next()
            for q in range(2):
                fc = 2 * fp + q
                wsrc, wtok, wc = (wup0, wup0.tok(), fc * 128) if fc < 8 else (wup, wup.tok(fc // 8), (fc - 8) * 128)
                for kc in range(8):
                    mm(psum[:, b, 256 * q:256 * q + N], wsrc.ap[:, kc, wc:wc + 128], h2.ap[:, kc, 0:N],
                       kc == 0, kc == 7, [wtok, h2.tok()], [PB[b]])
            r = rl.next()
            pv = psum[:, b, :].rearrange("p (q t) -> p q t", q=2)[:, :, 0:N]
            rv = r.ap.rearrange("p (q t) -> p q t", q=2)[:, :, 0:N]
            act(rv, pv, AF.Relu, [PB[b]], [r.tok()])
            tt("dve", hid.ap[:, 2 * fp:2 * fp + 2, 0:N], rv, rv, ALU.mult, [r.tok()], [hid.tok()])

    def ffn_down(gt, keep):
        o = 0
        for td, xt in zip(gt, keep):
            P = td["P"]
            ti = tiles.index(td)
            if td["s"] > 0:
                make_bcast(5, td["s"], 64, gab2)
            for half in range(2):
                for fc in range(32):
                    mm(psum[:P, 6 + half, :], hid.ap[:, fc, o:o + P], wdn.ap[:, fc, half * 512:half * 512 + 512],
                       fc == 0, fc == 31, [hid.tok(), wdn.tok(fc // 8)], [PB[6 + half]])
            post_add(td, ti, (6, 7), gab2, xt, td["y"], None, junk)
            o += P

    h2 = h2ring.next()
    keep = []
    cur = (h2, prenorm_group(groups[0], load_x1, h2, 3, 4, x1ring, xsring, junk, keep), keep)
    for g, gt in enumerate(groups):
        h2, N, keep = cur
        ffn_up(h2, N)
        if g + 1 < len(groups):
            h2n = h2ring.next()
            keepn = []
            cur = (h2n, prenorm_group(groups[g + 1], load_x1, h2n, 3, 4, x1ring, xsring, junk, keepn), keepn)
        ffn_down(gt, keep)

    return finish()


_CACHE = {}


def _prep_inputs(inp):
    constF, constB, alibi = make_consts()
    f = lambda a: np.ascontiguousarray(np.asarray(a, dtype=np.float32))
    w_ada, w_in, w_out = f(inp["w_ada"][0]), f(inp["w_in"][0]), f(inp["w_out"][0])
    w_up, w_down = f(inp["w_up"][0]), f(inp["w_down"][0])
    colp = lambda v, n: f(v).reshape(n, 128).T
    lam = np.concatenate([np.broadcast_to(f(inp[k][0])[None, :], (128, 64))
                          for k in ("lambda_q1", "lambda_k1", "lambda_q2", "lambda_k2")], axis=1)
    cols = np.concatenate([colp(inp["b_ada"][0], 48), colp(inp["g_pre_mix"][0], 8), colp(inp["g_post_mix"][0], 8),
                           colp(inp["g_pre_ffn"][0], 8), colp(inp["g_post_ffn"][0], 8),
                           f(inp["diff_subln_g"][0]).reshape(128, 1), lam, make_kbias()], axis=1)
    cols = f(cols)
    maps = []
    for c in range(8):
        c3 = np.stack([inp["c_prompt"][c], inp["c_sample"][2 * c], inp["c_sample"][2 * c + 1]])
        m = dict(
            xp=f(inp["x_prompt"][c]), xs=f(inp["x_sample"][2 * c:2 * c + 2]).reshape(128, D),
            cT=f(f(c3).reshape(3, 8, 128).transpose(2, 1, 0)),
            cdk=f(inp["cache_diff_k"][0, 2 * c:2 * c + 2]).reshape(2, PAST, 512),
            cdv=f(inp["cache_diff_v"][0, 2 * c:2 * c + 2]).reshape(2, PAST, 512),
            csk=f(inp["cache_sb_k"][0, 2 * c:2 * c + 2]).reshape(2, PAST, 512),
            csv=f(inp["cache_sb_v"][0, 2 * c:2 * c + 2]).reshape(2, PAST, 512),
            w_ada=w_ada, w_in=w_in, w_out=w_out, w_up=w_up, w_down=w_down,
            cols=cols, constF=constF, constB=constB, alibi=alibi)
        maps.append(m)
    return maps


def kernel(**inp):
    if "nc" not in _CACHE:
        _CACHE["nc"], _CACHE["info"] = build_program()
    nc = _CACHE["nc"]
    maps = _prep_inputs(inp)
    res = run_bass_kernel_spmd(nc, maps, core_ids=list(range(8)))
    R = res.results
    y_prompt = np.stack([R[c]["yp"] for c in range(8)])
    y_sample = np.concatenate([R[c]["ys"].reshape(2, NS, D) for c in range(8)])
    outs = [y_prompt, y_sample]
    for n, hd, dd in (("kdp", 4, 128), ("vdp", 4, 128), ("ksp", 8, 64), ("vsp", 8, 64)):
        outs.append(np.stack([R[c][n].reshape(SEQ, hd, dd) for c in range(8)])[None])
    for n, hd, dd in (("kds", 4, 128), ("vds", 4, 128), ("kss", 8, 64), ("vss", 8, 64)):
        outs.append(np.concatenate([R[c][n].reshape(2, NS, hd, dd) for c in range(8)])[None])
    return tuple(np.ascontiguousarray(o, dtype=np.float32) for o in outs)
```

```python
import math
import numpy as np
from contextlib import ExitStack

import concourse.bass as bass
import concourse.mybir as mybir
from concourse.bass_utils import run_bass_kernel_spmd

F32 = mybir.dt.float32
BF16 = mybir.dt.bfloat16
U8 = mybir.dt.uint8
AF = mybir.ActivationFunctionType
ALU = mybir.AluOpType

D = 1024
SEQ = 2048
NS = 64
PAST = 2048
EPS = 1e-6
LAM_INIT = 0.8 - 0.6 * math.exp(-0.3 * 0)
SLOPES = [2.0 ** (-8.0 * (h + 1) / 4) for h in range(4)]
NEGM = -30000.0
ARENA_BYTES = 206 * 1024


class Tok:
    __slots__ = ("last_w", "readers")

    def __init__(self, hist=None):
        self.last_w = None
        self.readers = dict(hist) if hist else {}


class _Op:
    __slots__ = ("id", "eng", "key", "fn", "deps", "signal", "dma", "sem", "val")


class Sched:
    def __init__(self, nc, n_sp=16, n_pool=8):
        self.nc = nc
        self.ops = []
        self.engs = {"pe": nc.tensor, "act": nc.scalar, "dve": nc.vector, "pool": nc.gpsimd, "sp": nc.sync}
        self.nslots = {"sp": n_sp, "pool": n_pool, "act": 0}
        self.dcount = {"sp": 0, "pool": 0}
        self.slot_prev = {}

    def sem_names(self):
        names = ["pe", "act", "dve", "pool"]
        for q in ("sp", "pool"):
            names += [("dma", q, i) for i in range(self.nslots[q])]
        return names

    def _new(self, eng, fn, dma):
        op = _Op()
        op.id = len(self.ops)
        op.eng = eng
        op.fn = fn
        op.deps = set()
        op.signal = False
        op.dma = dma
        op.sem = None
        op.val = 0
        op.key = eng
        return op

    def _dep(self, op, pid, raw):
        prod = self.ops[pid]
        if not prod.dma and not op.dma and prod.eng == op.eng:
            if not raw or op.eng == "pe":
                return
        op.deps.add(pid)

    def _add(self, op, reads, writes):
        for t in reads:
            if t.last_w is not None:
                self._dep(op, t.last_w, True)
        for t in writes:
            if t.last_w is not None:
                self._dep(op, t.last_w, False)
            for r in t.readers.values():
                self._dep(op, r, False)
        for t in reads:
            t.readers[op.key] = op.id
        for t in writes:
            t.last_w = op.id
            t.readers = {}
        self.ops.append(op)
        return op

    def op(self, eng, fn, reads=(), writes=()):
        return self._add(self._new(eng, fn, False), reads, writes)

    def dma(self, q, fn, reads=(), writes=()):
        op = self._new(q, fn, True)
        slot = self.dcount[q] % self.nslots[q]
        self.dcount[q] += 1
        op.key = op.sem = ("dma", q, slot)
        if op.key in self.slot_prev:
            op.deps.add(self.slot_prev[op.key])
        self.slot_prev[op.key] = op.id
        return self._add(op, reads, writes)

    def emit(self, sems):
        ops = self.ops
        for op in ops:
            for d in op.deps:
                ops[d].signal = True
        counts = {}
        for op in ops:
            if op.dma:
                counts[op.sem] = counts.get(op.sem, 0) + 16
                op.val = counts[op.sem]
                op.signal = True
            elif op.signal:
                op.sem = op.eng
                counts[op.eng] = counts.get(op.eng, 0) + 1
                op.val = counts[op.eng]
        waited = {e: {} for e in self.engs}
        nw = 0
        for op in ops:
            e = self.engs[op.eng]
            w = waited[op.eng]
            need = {}
            for d in op.deps:
                p = ops[d]
                if need.get(p.sem, 0) < p.val:
                    need[p.sem] = p.val
            for key, val in need.items():
                if w.get(key, 0) >= val:
                    continue
                w[key] = val
                e.wait_ge(sems[key], val)
                nw += 1
            ins = op.fn(e)
            if op.signal:
                ins.then_inc(sems[op.sem], 16 if op.dma else 1)
        self.n_wait = nw
        sp = self.engs["sp"]
        for key, val in counts.items():
            if isinstance(key, tuple):
                sp.wait_ge(sems[key], val)


class Buf:
    def __init__(self, arena, off, nbytes, hist, ap):
        self.arena, self.off, self.nbytes, self.hist, self.ap = arena, off, nbytes, hist, ap
        self.toks = {}

    def tok(self, key=None):
        t = self.toks.get(key)
        if t is None:
            t = self.toks[key] = Tok(self.hist)
        return t

    def __getitem__(self, idx):
        return self.ap[idx]


class Arena:
    def __init__(self, sched, ap, total):
        self.S, self.ap, self.total = sched, ap, total
        self.free = [[0, total, {}]]
        self.peak = 0
        self.used = 0

    def alloc(self, nbytes, dt, pat=None, **kw):
        req = nbytes
        nbytes = (nbytes + 31) // 32 * 32
        fl = self.free
        for i in range(len(fl)):
            j, end, hist = i, fl[i][1], dict(fl[i][2])
            while end - fl[i][0] < nbytes and j + 1 < len(fl) and fl[j + 1][0] == end:
                j += 1
                end = fl[j][1]
                self._merge(hist, fl[j][2])
            if end - fl[i][0] >= nbytes:
                start = fl[i][0]
                rest = [[start + nbytes, end, dict(hist)]] if end > start + nbytes else []
                fl[i:j + 1] = rest
                a = self.ap[:, start:start + req].bitcast(dt)
                if pat:
                    a = a.rearrange(pat, **kw)
                self.used += nbytes
                self.peak = max(self.peak, self.used)
                return Buf(self, start, nbytes, hist, a)
        raise MemoryError(f"arena: cannot alloc {nbytes} (used {self.used})")

    def _merge(self, hist, other):
        for k, v in other.items():
            if hist.get(k, -1) < v:
                hist[k] = v

    def release(self, buf):
        hist = dict(buf.hist)
        ops = self.S.ops
        for t in buf.toks.values():
            if t.last_w is not None:
                self._merge(hist, {ops[t.last_w].key: t.last_w})
            self._merge(hist, t.readers)
        self.used -= buf.nbytes
        fl = self.free
        fl.append([buf.off, buf.off + buf.nbytes, hist])
        fl.sort(key=lambda b: b[0])


class Ring:
    def __init__(self, bufs):
        self.bufs, self.i = bufs, 0

    def next(self):
        b = self.bufs[self.i % len(self.bufs)]
        self.i += 1
        return b


def pipeline(items, stages):
    n, k = len(items), len(stages)
    for step in range(n + k - 1):
        for s in range(k - 1, -1, -1):
            i = step - s
            if 0 <= i < n:
                stages[s](items[i])


NCOL = 48 + 32 + 1 + 256 + 51


def make_kbias():
    k = np.arange(128, dtype=np.float64)[:, None]
    n = np.arange(17, dtype=np.float64)[None, :]
    return np.concatenate([SLOPES[h] * (k + 128.0 * (n - 16.0)) for h in (1, 2, 3)], axis=1).astype(np.float32)


def make_consts():
    i = np.arange(128)
    ident = np.eye(128, dtype=np.float32)
    ones = np.ones((128, 128), np.float32)
    negtri = -(i[:, None] >= i[None, :]).astype(np.float32)
    sbmask = np.where(i[:, None] < i[None, :], 0.0, NEGM).astype(np.float32)
    dbs = []
    for h in range(4):
        vis = (i[:, None] // 64) <= (i[None, :] // 64)
        dbs.append(np.where(vis, -SLOPES[h] * np.abs(i[None, :] - i[:, None]), NEGM).astype(np.float32))
    constF = np.concatenate([ident, ones], axis=1)
    nt64 = np.zeros((128, 64), np.float32)
    nt64[:64, :] = negtri[:64, :64]
    dbs2 = [np.concatenate([d_, d_], axis=1) for d_ in dbs]
    akneg = np.zeros((128, 128), np.float32)
    akneg[0:2, :] = -1.0
    constB = np.concatenate([ident, ones, negtri, sbmask, sbmask] + dbs2 + [nt64, akneg], axis=1)
    q = np.arange(512)
    k = np.arange(128)
    al = np.zeros((3, 4 * 512 + 4 * 128), np.float32)
    q = np.concatenate([np.arange(256), np.arange(256)])
    for h in range(4):
        al[0, h * 512:(h + 1) * 512] = -SLOPES[h] * 16 * (q // 16)
        al[1, h * 512:(h + 1) * 512] = -SLOPES[h] * (q % 16)
        al[2, h * 512:(h + 1) * 512] = 1.0
        o = 2048 + h * 128
        al[0, o:o + 128] = 1.0
        al[1, o:o + 128] = 1.0
        al[2, o:o + 128] = SLOPES[h] * k
    return constF, constB, al


def build_program(dbg=False, stop=99):
    nc = bass.Bass("TRN2", target_bir_lowering=False)

    def din(name, shape):
        return nc.dram_tensor(name, shape, F32, kind="ExternalInput").ap()

    def dout(name, shape):
        return nc.dram_tensor(name, shape, F32, kind="ExternalOutput").ap()

    xp = din("xp", [SEQ, D])
    xs = din("xs", [2 * NS, D])
    cT = din("cT", [128, 8, 3])
    cdk = din("cdk", [2, PAST, 512])
    cdv = din("cdv", [2, PAST, 512])
    csk = din("csk", [2, PAST, 512])
    csv = din("csv", [2, PAST, 512])
    w_ada = din("w_ada", [D, 6 * D])
    w_in = din("w_in", [D, 3 * D])
    w_out = din("w_out", [D, D])
    w_up = din("w_up", [D, 4 * D])
    w_down = din("w_down", [4 * D, D])
    cols_d = din("cols", [128, NCOL])
    constF_d = din("constF", [128, 256])
    constB_d = din("constB", [128, 1856])
    alibi_d = din("alibi", [3, 2560])
    yp = dout("yp", [SEQ, D])
    ys = dout("ys", [2 * NS, D])
    kvp = [dout(n, [SEQ, 512]) for n in ("kdp", "vdp", "ksp", "vsp")]
    kvs = [dout(n, [2 * NS, 512]) for n in ("kds", "vds", "kss", "vss")]

    es = ExitStack()
    arena_t = es.enter_context(nc.sbuf_tensor("arena", [128, ARENA_BYTES], U8))
    psum = es.enter_context(nc.psum_tensor("psum", [128, 8, 512], F32))
    S = Sched(nc)
    sems = {n: es.enter_context(nc.semaphore("s_" + "_".join(str(x) for x in (n if isinstance(n, tuple) else (n,)))))
            for n in S.sem_names()}
    A = Arena(S, arena_t, ARENA_BYTES)
    PB = [Tok() for _ in range(8)]

    def finish():
        S.emit(sems)
        es.close()
        return nc, dict(ops=len(S.ops), waits=S.n_wait, peak=A.peak)

    def bv(ap2d):
        return ap2d.rearrange("p (m n) -> p m n", m=2)

    def mm(out, lhsT, rhs, start, stop, reads, writes):
        S.op("pe", lambda e: e.matmul(out, lhsT=lhsT, rhs=rhs, start=start, stop=stop), reads, writes)

    def tr(out, in_, ident, reads, writes):
        S.op("pe", lambda e: e.transpose(out, in_, ident), reads, writes)

    def act(out, in_, func, reads, writes, bias=0.0, scale=1.0, accum=None):
        S.op("act", lambda e: e.activation(out=out, in_=in_, func=func, bias=bias, scale=scale, accum_out=accum),
             reads, writes)

    def ts(eng, out, in0, s1, op0, reads, writes, s2=None, op1=None):
        if op1 is None:
            S.op(eng, lambda e: e.tensor_scalar(out=out, in0=in0, scalar1=s1, scalar2=None, op0=op0), reads, writes)
        else:
            S.op(eng, lambda e: e.tensor_scalar(out=out, in0=in0, scalar1=s1, scalar2=s2, op0=op0, op1=op1),
                 reads, writes)

    def tt(eng, out, in0, in1, op, reads, writes):
        S.op(eng, lambda e: e.tensor_tensor(out=out, in0=in0, in1=in1, op=op), reads, writes)

    def stt(eng, out, in0, scalar, in1, op0, op1, reads, writes):
        S.op(eng, lambda e: e.scalar_tensor_tensor(out=out, in0=in0, scalar=scalar, in1=in1, op0=op0, op1=op1),
             reads, writes)

    def cp(eng, out, in_, reads, writes):
        S.op(eng, lambda e: e.tensor_copy(out=out, in_=in_), reads, writes)

    def dma(q, out, in_, reads, writes):
        S.dma(q, lambda e: e.dma_start(out=out, in_=in_), reads, writes)

    cF = A.alloc(256 * 4, F32)
    cB = A.alloc(1856 * 2, BF16)
    zB = A.alloc(512 * 2, BF16)
    alb = A.alloc(2560 * 2, BF16)
    cols = A.alloc(NCOL * 4, F32)
    modc = A.alloc(6 * 8 * 3 * 4, F32, "p (v c s) -> p v c s", v=6, c=8)
    small = A.alloc(64 * 4, F32)
    stat = A.alloc(64 * 4, F32)
    junk = A.alloc(2048, BF16)
    dgr = Ring([A.alloc(512, F32) for _ in range(2)])
    dma("sp", cF.ap, constF_d, [], [cF.tok()])
    dma("pool", cB.ap, constB_d, [], [cB.tok()])
    S.op("pool", lambda e: e.memset(alb.ap, 0.0), [], [alb.tok()])
    dma("pool", alb.ap[0:3, :], alibi_d, [], [alb.tok()])
    dma("sp", cols.ap, cols_d, [], [cols.tok()])
    S.op("pool", lambda e: e.memset(zB.ap, 0.0), [], [zB.tok()])
    ident_f, ones_f = cF.ap[:, 0:128], cF.ap[:, 128:256]
    ident_b, ones_b = cB.ap[:, 0:128], cB.ap[:, 128:256]
    negtri_b = cB.ap[:, 256:384]
    sbmask2 = cB.ap[:, 384:640].rearrange("p (m n) -> p m n", m=2)
    dbias2 = [cB.ap[:, 640 + 256 * h: 896 + 256 * h].rearrange("p (m n) -> p m n", m=2) for h in range(4)]
    negtri64_b = cB.ap[:, 1664:1728]
    akneg_b = cB.ap[:, 1728:1856]
    aq2 = [alb.ap[:, 512 * h: 512 * (h + 1)].rearrange("p (m n) -> p m n", m=2) for h in range(4)]
    ak = [alb.ap[:, 2048 + 128 * h: 2048 + 128 * (h + 1)] for h in range(4)]
    C_BADA, C_G = 0, 48
    C_SUBLN, C_LAM = 80, 81
    C_KB = 337
    neglam = small.ap[:, 0:1]
    gsub = small.ap[:, 1:2]

    lt = A.alloc(128 * 4, F32)
    tsm = small.tok()
    for i in range(2):
        lq = cols.ap[:, C_LAM + 128 * i: C_LAM + 128 * i + 64]
        lk = cols.ap[:, C_LAM + 128 * i + 64: C_LAM + 128 * i + 128]
        tt("dve", lt.ap[:, 64 * i:64 * i + 64], lq, lk, ALU.mult, [cols.tok()], [lt.tok()])
        S.op("dve", lambda e, i=i: e.reduce_sum(out=small.ap[:, 2 + i:3 + i], in_=lt.ap[:, 64 * i:64 * i + 64],
                                                 axis=mybir.AxisListType.X), [lt.tok()], [tsm])
    act(small.ap[:, 4:6], small.ap[:, 2:4], AF.Exp, [tsm], [tsm])
    tt("dve", small.ap[:, 6:7], small.ap[:, 5:6], small.ap[:, 4:5], ALU.subtract, [tsm], [tsm])
    ts("dve", neglam, small.ap[:, 6:7], -LAM_INIT, ALU.add, [tsm], [tsm])
    ts("dve", gsub, cols.ap[:, C_SUBLN:C_SUBLN + 1], 1.0 - LAM_INIT, ALU.mult, [cols.tok()], [tsm])
    A.release(lt)

    cTf = A.alloc(24 * 4, F32, "p (c s) -> p c s", c=8)
    siluT = A.alloc(24 * 2, BF16, "p (c s) -> p c s", c=8)
    dma("sp", cTf.ap, cT, [], [cTf.tok()])
    act(siluT.ap, cTf.ap, AF.Silu, [cTf.tok()], [siluT.tok()])
    wada = [A.alloc(8 * 1024 * 2, BF16, "p (k n) -> p k n", k=8) for _ in range(2)]
    w_ada_v = w_ada.rearrange("(k p) n -> p k n", p=128)
    psmod = psum[:, 0, 0:144]
    tmv = [modc.tok(v) for v in range(6)]
    POSTG = {1: 0, 4: 2}
    POSTA = {2: 1, 5: 3}

    def mod_load(v, wb, j0=0, j1=8):
        dma("pool", wb.ap[:, :, 0:128 * (j1 - j0)], w_ada_v[:, :, v * 1024 + 128 * j0:v * 1024 + 128 * j1], [], [wb.tok()])

    def mod_block(v, wb, j0=0, j1=8):
        for j in range(j0, j1):
            for kc in range(8):
                mm(psmod[:, (v * 8 + j) * 3:(v * 8 + j) * 3 + 3], wb.ap[:, kc, (j - j0) * 128:(j - j0 + 1) * 128],
                   siluT.ap[:, kc, :], kc == 0, kc == 7, [wb.tok(), siluT.tok()], [PB[0]])
        pv = psmod[:, v * 24:(v + 1) * 24].rearrange("p (c s) -> p c s", c=8)
        for s in range(3):
            md = modc.ap[:, v, j0:j1, s]
            tt("dve", md, pv[:, j0:j1, s], cols.ap[:, C_BADA + v * 8 + j0: C_BADA + v * 8 + j1], ALU.add,
               [PB[0], cols.tok()], [tmv[v]])
            if v in POSTG:
                gi = POSTG[v]
                stt("dve", md, md, 1.0, cols.ap[:, C_G + 8 * gi + j0: C_G + 8 * gi + j1],
                    ALU.add, ALU.mult, [tmv[v], cols.tok()], [tmv[v]])
            if v in POSTA:
                gi = POSTA[v]
                tt("dve", md, md, cols.ap[:, C_G + 8 * gi + j0: C_G + 8 * gi + j1],
                   ALU.mult, [tmv[v], cols.tok()], [tmv[v]])

    win = A.alloc(8 * 3072 * 2, BF16, "p (k n) -> p k n", k=8)
    w_in_v = w_in.rearrange("(k p) n -> p k n", p=128)
    for v in range(2):
        mod_load(v, wada[v % 2])
        if v == 1:
            for blk in range(3):
                dma("pool", win.ap[:, :, blk * 1024:(blk + 1) * 1024], w_in_v[:, :, blk * 1024:(blk + 1) * 1024],
                    [], [win.tok(blk)])
        mod_block(v, wada[v % 2])
    for b in wada:
        A.release(b)

    if stop <= 0:
        return finish()
    tiles = [dict(s=0, P=128, col=128 * i, row=128 * i, x=xp, y=yp, kv=kvp) for i in range(16)]
    tiles += [dict(s=1 + b, P=64, col=2048 + 64 * b, row=64 * b, x=xs, y=ys, kv=kvs) for b in range(2)]
    ytok = [Tok() for _ in tiles]

    qTs = A.alloc(8 * 128 * 2, BF16, "p (c t) -> p c t", c=8)
    kTs = A.alloc(8 * 128 * 2, BF16, "p (c t) -> p c t", c=8)
    vBs = A.alloc(2 * 1024 * 2, BF16, "p (i n) -> p i n", i=2)
    qT = A.alloc(8 * 2048 * 2, BF16, "p (c t) -> p c t", c=8)
    kT = A.alloc(8 * 2048 * 2, BF16, "p (c t) -> p c t", c=8)
    vB = A.alloc(16 * 1024 * 2, BF16, "p (i n) -> p i n", i=16)

    stat_i = [0]

    def rms_rstd(parts, P, junk, n_feat, reads):
        k = stat_i[0] % 8
        stat_i[0] += 1
        t = stat.tok(k)
        base = stat.ap[:, 8 * k: 8 * k + 8]
        for i, pa in enumerate(parts):
            act(junk.ap[:P, 0:pa.shape[1]], pa, AF.Square, reads, [junk.tok(), t], accum=base[:P, i:i + 1])
        if len(parts) == 2:
            tt("dve", base[:P, 0:1], base[:P, 0:1], base[:P, 1:2], ALU.add, [t], [t])
        act(base[:P, 2:3], base[:P, 0:1], AF.Ln, [t], [t], bias=EPS, scale=1.0 / n_feat)
        act(base[:P, 3:4], base[:P, 2:3], AF.Exp, [t], [t], scale=-0.5)
        return base[:P, 3:4], t

    def prenorm_group(gt, load_x, hT, vsh, vG, xring, xsring, junk, keep=None):
        off = 0
        for td in gt:
            P = td["P"]
            xt = xring.next()
            load_x(td, xt)
            rstd, tst = rms_rstd([xt.ap[:P, :]], P, junk, D, [xt.tok()])
            xsb = xsring.next()
            ts("dve", xsb.ap[:P, :], xt.ap[:P, :], rstd, ALU.mult, [xt.tok(), tst], [xsb.tok()])
            for c in range(8):
                pv = psum[:, c // 2, (c % 2) * 256:(c % 2) * 256 + 256]
                tr(pv[:, off:off + P], xsb.ap[:P, c * 128:(c + 1) * 128], ident_f[:P, :P],
                   [xsb.tok(), cF.tok()], [PB[c // 2]])
            if keep is not None:
                keep.append(xt)
            off += P
        for c in range(8):
            pv = psum[:, c // 2, (c % 2) * 256:(c % 2) * 256 + 256]
            o = 0
            for td in gt:
                P, s = td["P"], td["s"]
                act(hT.ap[:, c, o:o + P], pv[:, o:o + P], AF.Identity, [PB[c // 2], tmv[vsh], tmv[vG]], [hT.tok()],
                    bias=modc.ap[:, vsh, c, s:s + 1], scale=modc.ap[:, vG, c, s:s + 1])
                o += P
        return off

    xring = Ring([A.alloc(4096, F32) for _ in range(3)])
    xsring = Ring([A.alloc(4096, F32) for _ in range(1)])
    hring = Ring([A.alloc(8 * 256 * 2, BF16, "p (c t) -> p c t", c=8) for _ in range(2)])
    stg = Ring([A.alloc(2048, F32) for _ in range(4)])
    groups = [tiles[2 * g:2 * g + 2] for g in range(8)] + [tiles[16:18]]
    pring = Ring([4, 5, 6, 7])
    evq = [0]
    FM = [(0, 0, 0.125), (0, 1, 0.125), (0, 2, 0.125), (0, 3, 0.125), (1, 0, 1.0), (1, 1, 1.0), (1, 2, 1.0), (1, 3, 1.0),
          (0, 4, 0.125), (0, 5, 0.125), (0, 6, 0.125), (0, 7, 0.125), (1, 4, 1.0), (1, 5, 1.0), (1, 6, 1.0), (1, 7, 1.0)]
    FMCOL = [0, 128, 256, 384, 512, 640, 768, 896, 1536, 1664, 1792, 1920, 2048, 2176, 2304, 2432]

    def load_x0(td, xt):
        dma("sp", xt.ap[:td["P"], :], td["x"][td["row"]:td["row"] + td["P"], :], [], [xt.tok()])

    def proj_fm(g, gt, hT, N):
        c0 = gt[0]["col"]
        for oc in range(16):
            which, idx, scl = FM[oc]
            wc = FMCOL[oc]
            b = pring.next()
            for kc in range(8):
                mm(psum[:, b, 0:N], win.ap[:, kc, wc:wc + 128], hT.ap[:, kc, 0:N], kc == 0, kc == 7,
                   [win.tok(wc // 1024), hT.tok()], [PB[b]])
            if g < 8:
                dst, dc0 = (qT if which == 0 else kT), c0
            else:
                dst, dc0 = (qTs if which == 0 else kTs), c0 - 2048
            eng = "act" if evq[0] % 2 == 0 else "dve"
            evq[0] += 1
            if eng == "act":
                act(dst.ap[:, idx, dc0:dc0 + N], psum[:, b, 0:N], AF.Identity, [PB[b]], [dst.tok((idx, g))], scale=scl)
            else:
                ts("dve", dst.ap[:, idx, dc0:dc0 + N], psum[:, b, 0:N], scl, ALU.mult, [PB[b]], [dst.tok((idx, g))])

    def proj_tm(g, gt, hT):
        o = 0
        for td in gt:
            P = td["P"]
            ti = tiles.index(td)
            for bi, wc in enumerate((512, 1024, 2048, 2560)):
                b = pring.next()
                for kc in range(8):
                    mm(psum[:P, b, :], hT.ap[:, kc, o:o + P], win.ap[:, kc, wc:wc + 512], kc == 0, kc == 7,
                       [win.tok(wc // 1024), hT.tok()], [PB[b]])
                sb_ = stg.next()
                eng = "act" if evq[0] % 2 == 0 else "dve"
                evq[0] += 1
                if eng == "act":
                    act(sb_.ap[:P, :], psum[:P, b, :], AF.Identity, [PB[b]], [sb_.tok()])
                else:
                    cp("dve", sb_.ap[:P, :], psum[:P, b, :], [PB[b]], [sb_.tok()])
                dma("sp", td["kv"][bi][td["row"]:td["row"] + P, :], sb_.ap[:P, :], [sb_.tok()], [Tok()])
                if bi in (1, 3):
                    h = 0 if bi == 1 else 1
                    vdst, vti = (vB, ti) if ti < 16 else (vBs, ti - 16)
                    cp("pool" if P == 128 else "dve", vdst.ap[:P, vti, 512 * h:512 * h + 512], sb_.ap[:P, :],
                       [sb_.tok()], [vdst.tok((vti, h))])
            o += P

    hT = hring.next()
    cur = (hT, prenorm_group(groups[0], load_x0, hT, 0, 1, xring, xsring, junk))
    for g, gt in enumerate(groups):
        hT, N = cur
        proj_fm(g, gt, hT, N)
        if g + 1 < len(groups):
            hTn = hring.next()
            cur = (hTn, prenorm_group(groups[g + 1], load_x0, hTn, 0, 1, xring, xsring, junk))
        proj_tm(g, gt, hT)
    for r in (xring, xsring, hring, stg):
        for b in r.bufs:
            A.release(b)
    A.release(win)

    if stop <= 1:
        return finish()
    wout = A.alloc(8 * 1024 * 2, BF16, "p (k n) -> p k n", k=8)
    dma("pool", wout.ap, w_out.rearrange("(k p) n -> p k n", p=128), [], [wout.tok()])
    gab = A.alloc(4096, F32)
    mixT = A.alloc(8 * 512 * 2, BF16, "p (c t) -> p c t", c=8)
    ering = Ring([A.alloc(1024, BF16) for _ in range(4)])
    fring = Ring([A.alloc(2048, F32) for _ in range(4)])
    for bfr in ering.bufs + fring.bufs:
        S.op("pool", lambda e, bfr=bfr: e.memset(bfr.ap, 0.0), [], [bfr.tok()])
    qzr = Ring([A.alloc(1024, BF16) for _ in range(3)])
    for bfr in qzr.bufs:
        S.op("pool", lambda e, bfr=bfr: e.memset(bfr.ap, 0.0), [], [bfr.tok()])
    ftmps = [[A.alloc(2048, F32) for _ in range(3)] for _ in range(2)]
    ftmp = ftmps[0] + ftmps[1]
    unit_par = [0]

    def make_qz(qsrc, idx, qc0, N, rq):
        qz = qzr.next()
        qzv = bv(qz.ap)
        cp("dve", qzv[0:64, 0, 0:N], qsrc.ap[0:64, idx, qc0:qc0 + N], rq, [qz.tok()])
        cp("dve", qzv[64:128, 1, 0:N], qsrc.ap[64:128, idx, qc0:qc0 + N], rq, [qz.tok()])
        return qz, qzv

    Rb = [A.alloc(2048, F32) for _ in range(2)]
    xring = Ring([A.alloc(4096, F32) for _ in range(2)])
    tring = Ring([A.alloc(4096, F32) for _ in range(2)])

    def make_bcast(v, s, P, dst):
        for c in range(8):
            dg = dgr.next()
            ts("dve", dg.ap, ident_f, modc.ap[:, v, c, s:s + 1], ALU.mult, [cF.tok(), tmv[v]], [dg.tok()])
            b = 6 + c // 4
            mm(psum[:P, b, (c % 4) * 128:(c % 4) * 128 + 128], ones_f[:, :P], dg.ap, True, True,
               [cF.tok(), dg.tok()], [PB[b]])
        for hb in range(2):
            cp("dve", dst.ap[:P, 512 * hb:512 * hb + 512], psum[:P, 6 + hb, :], [PB[6 + hb]], [dst.tok()])

    def post_add(td, ti, ybanks, gab_, xt, out_dram, xtok_reads, junk_):
        P = td["P"]
        yv = psum[:P, ybanks[0]:ybanks[0] + 2, :]
        rstd, tst = rms_rstd([yv[:, 0, :], yv[:, 1, :]], P, junk_, D, [PB[ybanks[0]], PB[ybanks[1]]])
        t = tring.next()
        for hb in range(2):
            stt("dve", t.ap[:P, 512 * hb:512 * hb + 512], yv[:, hb, :], rstd, gab_.ap[:P, 512 * hb:512 * hb + 512],
                ALU.mult, ALU.mult, [PB[ybanks[hb]], tst, gab_.tok()], [t.tok()])
        tt("pool" if P == 128 else "dve", t.ap[:P, :], t.ap[:P, :], xt.ap[:P, :], ALU.add, [t.tok(), xt.tok()], [t.tok()])
        dma("sp", out_dram[td["row"]:td["row"] + P, :], t.ap[:P, :], [t.tok()], [ytok[ti]])

    from collections import deque
    pending = deque()

    def stream(items, nstages, stage_fn, pre_stage=None):
        n = len(items)
        for step in range(n + nstages - 1):
            if pre_stage is not None and 0 <= step - pre_stage < n:
                stage_fn(-pre_stage, items[step - pre_stage])
            for s_ in range(nstages - 1, -1, -1):
                i = step - s_
                if 0 <= i < n:
                    stage_fn(s_, items[i])
            if pending:
                pending.popleft()[1]()

    def flush_pending(par=None):
        if par is None:
            while pending:
                pending.popleft()[1]()
            return
        last = -1
        for i_, (p_, _) in enumerate(pending):
            if p_ == par:
                last = i_
        for _ in range(last + 1):
            pending.popleft()[1]()

    def diff_stream(N, qsrc, groups_):
        sring = Ring([4, 5])
        units = []
        for gd in groups_:
            for hd in gd["heads"]:
                par = unit_par[0] % 2
                unit_par[0] += 1
                units.append(dict(hd=hd, par=par, OB=2 * par, DB=2 * par + 1, qz=None,
                                  kts=gd["kts"], qc0=gd["qc0"], moff=gd["moff"]))
        items = [(u, kt) for u in units for kt in u["kts"]]

        def mkq(u):
            u["qz"] = make_qz(qsrc, u["hd"], u["qc0"], N, [qsrc.tok(k) for k in u["kts"][0]["qtoks"](u["hd"])])
        mkq(units[0])
        state = {}

        def finalizers(u):
            hd, par = u["hd"], u["par"]
            Ov, Dv = bv(psum[:, u["OB"], :]), bv(psum[:, u["DB"], :])
            ta, tb, tc = ftmps[par]
            tav, tbv = bv(ta.ap), bv(tb.ap)
            mix_dst, mix_tok = mixT.ap[:, hd, u["moff"]:u["moff"] + N], mixT.tok(u["moff"])
            fo = 256 * par

            def f1():
                act(tav[:, :, :N], Dv[:, :, :N], AF.Ln, [PB[u["DB"]]], [ta.tok()])
                act(tav[:, :, :N], tav[:, :, :N], AF.Exp, [ta.tok()], [ta.tok()], scale=-1.0)

            def f2():
                tt("dve", tbv[:, :, :N], Ov[:, :, :N], tav[:, :, :N], ALU.mult, [PB[u["OB"]], ta.tok()], [tb.tok()])
                stt("dve", tc.ap[:, :N], tbv[:, 1, :N], neglam, tbv[:, 0, :N], ALU.mult, ALU.add,
                    [tb.tok(), tsm], [tc.tok()])

            thl = ta.ap[:, 256:512].bitcast(BF16)

            def f3():
                tt("dve", ta.ap[:, :N], tc.ap[:, :N], tc.ap[:, :N], ALU.mult, [tc.tok()], [ta.tok()])
                cp("dve", thl[:, 0:N], ta.ap[:, :N], [ta.tok()], [ta.tok()])
                tt("dve", thl[:, 256:256 + N], ta.ap[:, :N], thl[:, 0:N], ALU.subtract, [ta.tok()], [ta.tok()])

            fbs = {}

            def f4():
                FB = fbs["b"] = sring.next()
                mm(psum[:, FB, fo:fo + N], ones_b, thl[:, 0:N], True, False, [cB.tok(), ta.tok()], [PB[FB]])
                mm(psum[:, FB, fo:fo + N], ones_b, thl[:, 256:256 + N], False, True, [cB.tok(), ta.tok()], [PB[FB]])

            def f5():
                FB = fbs["b"]
                act(tb.ap[:, :N], psum[:, FB, fo:fo + N], AF.Ln, [PB[FB]], [tb.tok()], bias=EPS, scale=1.0 / 128)
                act(tb.ap[:, :N], tb.ap[:, :N], AF.Exp, [tb.tok()], [tb.tok()], scale=-0.5)

            def f6():
                stt("dve", mix_dst, tc.ap[:, :N], gsub, tb.ap[:, :N], ALU.mult, ALU.mult,
                    [tb.tok(), tc.tok(), tsm], [mix_tok])
            return [f1, f2, f3, f4, f5, f6]

        def stage(s_, it):
            u, kt = it
            hd = u["hd"]
            nk, c0, nd = kt["nk"], kt["col0"], kt["nd"]
            lin = c0 + nd < N
            if s_ == 0:
                if kt is u["kts"][0]:
                    ui = units.index(u)
                    if ui + 1 < len(units):
                        mkq(units[ui + 1])
                qz, qzv = u["qz"]
                b = sring.next()
                Sv = bv(psum[:, b, :])
                if hd == 0:
                    mm(Sv[:nk, :, c0:N], kt["kap"](hd), qzv[:, :, c0:N], True, False, [qz.tok()] + kt["ktoks"](hd), [PB[b]])
                    if nd:
                        mm(Sv[:nk, :, c0:c0 + nd], ident_b[:, :nk], dbias2[hd][:, :, :nd], False, not lin, [cB.tok()], [PB[b]])
                    if lin:
                        mm(Sv[:nk, :, c0 + nd:N], ak[hd][:, :nk], aq2[hd][:, :, c0 + nd:N], False, True, [alb.tok()], [PB[b]])
                else:
                    mm(Sv[:nk, :, c0:N], kt["kap"](hd), qzv[:, :, c0:N], True, not nd, [qz.tok()] + kt["ktoks"](hd), [PB[b]])
                    if nd:
                        mm(Sv[:nk, :, c0:c0 + nd], ident_b[:, :nk], dbias2[hd][:, :, :nd], False, False, [cB.tok()], [PB[b]])
                        mm(Sv[:nk, :, c0:c0 + nd], akneg_b[:, :nk], aq2[hd][:, :, c0:c0 + nd], False, True,
                           [cB.tok(), alb.tok()], [PB[b]])
                state[(id(u), id(kt))] = b
            elif s_ == 1:
                b = state[(id(u), id(kt))]
                Sv = bv(psum[:, b, :])
                E = ering.next()
                Ev = bv(E.ap)
                if nd:
                    act(Ev[:nk, :, c0:c0 + nd], Sv[:nk, :, c0:c0 + nd], AF.Exp, [PB[b]], [E.tok()])
                if lin:
                    if hd == 0:
                        act(Ev[:nk, :, c0 + nd:N], Sv[:nk, :, c0 + nd:N], AF.Exp, [PB[b]], [E.tok()],
                            bias=float(kt["const"] * SLOPES[hd]))
                    else:
                        kc_ = C_KB + (hd - 1) * 17 + (kt["const"] // 128 + 16)
                        act(Ev[:nk, :, c0 + nd:N], Sv[:nk, :, c0 + nd:N], AF.Exp, [PB[b], cols.tok()], [E.tok()],
                            bias=cols.ap[:nk, kc_:kc_ + 1])
                state[(id(u), id(kt))] = E
            else:
                E = state.pop((id(u), id(kt)))
                Ev = bv(E.ap)
                Ov, Dv = bv(psum[:, u["OB"], :]), bv(psum[:, u["DB"], :])
                first, last = kt is u["kts"][0], kt is u["kts"][-1]
                if first:
                    flush_pending(u["par"])
                mm(Ov[:, :, c0:N], kt["v"](hd), Ev[:nk, :, c0:N], first, last, [E.tok()] + kt["vtok"](0), [PB[u["OB"]]])
                mm(Dv[:, :, c0:N], ones_b[:nk, :], Ev[:nk, :, c0:N], first, last, [E.tok(), cB.tok()], [PB[u["DB"]]])
                if last:
                    pending.extend((u["par"], f_) for f_ in finalizers(u))

        if min(len(gd["kts"]) for gd in groups_) >= 2:
            stream(items, 3, stage)
        else:
            for u in units:
                stream([(u, kt) for kt in u["kts"]], 3, stage)
                flush_pending()

    def sb_stream(N, qsrc, groups_):
        minlen = min(len(gd["kts"]) for gd in groups_)
        if minlen >= 5:
            obanks, aring = [0, 1], Ring([2, 3, 4, 7])
        else:
            assert minlen >= 2
            obanks, aring = [0, 1, 7], Ring([2, 3, 4])
        cring = Ring([5, 6])
        units = []
        for gd in groups_:
            for j in gd["heads"]:
                par = unit_par[0] % len(obanks)
                unit_par[0] += 1
                units.append(dict(j=j, par=par, OB=obanks[par], R=(Rb + [ftmps[1][0]])[par], qz=None,
                                  kts=gd["kts"], qc0=gd["qc0"], moff=gd["moff"]))
        items = [(u, kt) for u in units for kt in u["kts"]]

        def mkq(u):
            u["qz"] = make_qz(qsrc, 4 + u["j"], u["qc0"], N, [qsrc.tok(k) for k in u["kts"][0]["qtoks"](4 + u["j"])])
        mkq(units[0])
        state = {}

        def stage(s_, it):
            u, kt = it
            j, R, OB = u["j"], u["R"], u["OB"]
            Rv = bv(R.ap)
            Ov = bv(psum[0:64, OB, :])
            nk, c0, nd = kt["nk"], kt["col0"], kt["nd"]
            key = (id(u), id(kt))
            if s_ == 0:
                if kt is u["kts"][0]:
                    ui = units.index(u)
                    if ui + 1 < len(units):
                        mkq(units[ui + 1])
                    flush_pending(u["par"])
                    S.op("pool", lambda e: e.memset(R.ap, 0.0), [], [R.tok()])
                    for hh in (0, 1):
                        mm(Ov[:, hh, 0:N], zB.ap[:, 0:64], zB.ap[:, 0:N], True, False, [zB.tok()], [PB[OB]])
                qz, qzv = u["qz"]
                b = aring.next()
                Av = bv(psum[:, b, :])
                mm(Av[:nk, :, c0:N], kt["kap"](4 + j), qzv[:, :, c0:N], True, False,
                   [qz.tok()] + kt["ktoks"](4 + j), [PB[b]])
                if nd:
                    mm(Av[:nk, :, c0:c0 + nd], ident_b[:, :nk], sbmask2[:, :, :nd], False, False, [cB.tok()], [PB[b]])
                state[key] = dict(a=b)
                return
            st = state[key]
            Av = bv(psum[:, st["a"], :])
            if s_ == -1:
                ef = fring.next()
                act(bv(ef.ap)[:nk, :, c0:N], Av[:nk, :, c0:N], AF.Exp, [PB[st["a"]]], [ef.tok()])
                st["ef"] = ef
            elif s_ == 1:
                ef = st["ef"]
                sp = ering.next()
                act(bv(sp.ap)[:nk, :, c0:N], bv(ef.ap)[:nk, :, c0:N], AF.Ln, [ef.tok()], [sp.tok()], bias=1.0)
                st["sp"] = sp
            elif s_ == 2:
                cb = cring.next()
                Cv = bv(psum[:, cb, :])
                spv = bv(st["sp"].ap)
                if nk == 128:
                    mm(Av[:, :, c0:N], negtri_b, spv[:, :, c0:N], False, True, [cB.tok(), st["sp"].tok()], [PB[st["a"]]])
                else:
                    mm(Av[:nk, :, c0:N], negtri64_b, spv[:, :, c0:N], False, True, [cB.tok(), st["sp"].tok()], [PB[st["a"]]])
                mm(Cv[:, :, c0:N], ones_b[:nk, :], spv[:nk, :, c0:N], True, True, [cB.tok(), st["sp"].tok()], [PB[cb]])
                st["c"] = cb
            elif s_ == 3:
                Cv = bv(psum[:, st["c"], :])
                bs = fring.next()
                tt("dve", bv(bs.ap)[:nk, :, c0:N], Av[:nk, :, c0:N], Rv[:nk, :, c0:N], ALU.subtract,
                   [PB[st["a"]], R.tok()], [bs.tok()])
                tt("dve", Rv[:, :, c0:N], Cv[:, :, c0:N], Rv[:, :, c0:N], ALU.add, [PB[st["c"]], R.tok()], [R.tok()])
                st["bs"] = bs
            elif s_ == 4:
                w = ering.next()
                act(bv(w.ap)[:nk, :, c0:N], bv(st["bs"].ap)[:nk, :, c0:N], AF.Exp, [st["bs"].tok()], [w.tok()])
                st["w"] = w
            else:
                state.pop(key)
                last = kt is u["kts"][-1]
                wv = bv(st["w"].ap)
                for hh in (0, 1):
                    mm(Ov[:, hh, c0:N], kt["v"](4 + j, hh), wv[:nk, hh, c0:N], False, last,
                       [st["w"].tok()] + kt["vtok"](1), [PB[OB]])
                if last:
                    mix_dst, mix_tok = mixT.ap[:, 4 + j, u["moff"]:u["moff"] + N], mixT.tok(u["moff"])

                    def fin():
                        cp("dve", mix_dst[0:64, :], Ov[:, 0, 0:N], [PB[OB]], [mix_tok])
                        cp("dve", mix_dst[64:128, :], Ov[:, 1, 0:N], [PB[OB]], [mix_tok])
                    pending.append((u["par"], fin))

        stream(items, 6, stage, pre_stage=1)

    def run_units(N, qsrc, qc0, kts, specs):
        run_groups(N, qsrc, [dict(qc0=qc0, kts=kts, specs=specs, moff=mix_off[0])])

    def run_groups(N, qsrc, gl, mid_fn=None):
        dg = [dict(qc0=g_["qc0"], kts=g_["kts"], moff=g_["moff"], heads=[i for k, i in g_["specs"] if k == "d"]) for g_ in gl]
        sg = [dict(qc0=g_["qc0"], kts=g_["kts"][::-1], moff=g_["moff"], heads=[i for k, i in g_["specs"] if k == "s"])
              for g_ in gl]
        dg = [x for x in dg if x["heads"]]
        sg = [x for x in sg if x["heads"]]
        if dg:
            diff_stream(N, qsrc, dg)
            flush_pending()
        if mid_fn is not None:
            mid_fn()
        if sg:
            sb_stream(N, qsrc, sg)
        flush_pending()

    mix_off = [0]

    def wout_closures(gt, gab_, junk_, moff):
        cl = []
        o = moff
        for td in gt:
            def f(td=td, o=o):
                P = td["P"]
                ti = tiles.index(td)
                xt = xring.next()
                dma("sp", xt.ap[:P, :], td["x"][td["row"]:td["row"] + P, :], [], [xt.tok()])
                for half in range(2):
                    for kc in range(8):
                        mm(psum[:P, 6 + half, :], mixT.ap[:, kc, o:o + P], wout.ap[:, kc, half * 512:half * 512 + 512],
                           kc == 0, kc == 7, [mixT.tok(moff), wout.tok()], [PB[6 + half]])
                post_add(td, ti, (6, 7), gab_, xt, td["y"], None, junk_)
            cl.append(f)
            o += td["P"]
        return cl

    def wout_post(gt, N, gab_, junk_):
        for f in wout_closures(gt, gab_, junk_, mix_off[0]):
            f()

    wlate = A.alloc(8 * 512 * 2, BF16, "p (k n) -> p k n", k=8)
    late = [(2, 0, 4), (2, 4, 8), (3, 0, 4), (3, 4, 8), (4, 0, 4), (4, 4, 8), (5, 0, 4), (5, 4, 8)]
    mod_load(late[0][0], wlate, late[0][1], late[0][2])

    def prompt_ktiles(g):
        kts = []
        for J in range(2 * g + 2):
            r = J - 2 * g
            c0 = 128 * r if r >= 0 else 0
            kts.append(dict(
                nk=128, col0=c0, nd=128 if r >= 0 else 0, const=-(256 * g - 128 * J),
                kap=lambda idx, J=J: kT.ap[:, idx, 128 * J:128 * J + 128],
                ktoks=lambda idx, J=J: [kT.tok((idx, J // 2))],
                qtoks=lambda idx, g=g: [(idx, g)],
                v=(lambda a, hh=None, J=J: vB.ap[:, J, 128 * a:128 * a + 128] if hh is None
                   else vB.ap[:, J, 512 + 64 * (2 * (a - 4) + hh): 512 + 64 * (2 * (a - 4) + hh) + 64]),
                vtok=lambda h, J=J: [vB.tok((J, h))]))
        return kts

    late_k = [0]

    def late_step():
        lv, l0, l1 = late[late_k[0]]
        mod_block(lv, wlate, l0, l1)
        late_k[0] += 1
        if late_k[0] < len(late):
            lv, l0, l1 = late[late_k[0]]
            mod_load(lv, wlate, l0, l1)
        else:
            A.release(wlate)

    allspec = [("d", h) for h in range(4)] + [("s", j) for j in range(4)]
    for p in range(4):
        gl = [dict(qc0=256 * g, kts=prompt_ktiles(g), specs=allspec, moff=256 * (g % 2)) for g in (2 * p, 2 * p + 1)]
        run_groups(256, qT, gl, mid_fn=late_step)
        late_step()
        if p == 0:
            make_bcast(2, 0, 128, gab)
        for g in (2 * p, 2 * p + 1):
            cl = wout_closures(tiles[2 * g:2 * g + 2], gab, junk, 256 * (g % 2))
            if p < 3:
                pending.extend((None, f_) for f_ in cl)
            else:
                for f_ in cl:
                    f_()

    flush_pending()
    mix_off[0] = 0
    if stop <= 2:
        return finish()
    for bfr in (qT, kT, vB):
        A.release(bfr)
    wup0 = A.alloc(8 * 1024 * 2, BF16, "p (k n) -> p k n", k=8)
    wupb = {0: wup0}
    ksts = [A.alloc(16 * 512 * 2, BF16, "p (i n) -> p i n", i=16) for _ in range(2)]
    vCs = [A.alloc(16 * 512 * 2, BF16, "p (i n) -> p i n", i=16) for _ in range(2)]
    kTc = A.alloc(4 * 2048 * 2, BF16, "p (c t) -> p c t", c=4)
    seqk = [(0, "d", cdk, cdv), (0, "s", csk, csv), (1, "d", cdk, cdv), (1, "s", csk, csv)]
    w_up_v = w_up.rearrange("(k p) n -> p k n", p=128)

    def cache_load(n):
        b_, _, ck_, cv_ = seqk[n]
        dma("pool", ksts[n % 2].ap, ck_[b_].rearrange("(i p) n -> p i n", p=128), [], [ksts[n % 2].tok()])
        dma("pool", vCs[n % 2].ap, cv_[b_].rearrange("(i p) n -> p i n", p=128), [], [vCs[n % 2].tok()])
    cache_load(0)
    cache_load(1)
    for b in range(2):
        td = tiles[16 + b]
        qc0 = 64 * b
        make_bcast(2, 1 + b, 64, gab)
        for kind, ck, cv in (("d", cdk, cdv), ("s", csk, csv)):
            n_ = 2 * b + (0 if kind == "d" else 1)
            kst, vC = ksts[n_ % 2], vCs[n_ % 2]
            for c in range(4):
                for half in range(2):
                    bnk = 4 + (2 * c + half) % 2
                    pb = psum[:, bnk, :].bitcast(BF16)
                    for i in range(8):
                        tr(pb[:, 128 * i:128 * i + 128], kst.ap[:, 8 * half + i, 128 * c:128 * c + 128], ident_b,
                           [kst.tok(), cB.tok()], [PB[bnk]])
                    if half:
                        cp("dve", kTc.ap[:, c, 1024:2048], pb, [PB[bnk]], [kTc.tok()])
                    else:
                        act(kTc.ap[:, c, 0:1024], pb, AF.Identity, [PB[bnk]], [kTc.tok()])
            if stop == 2.1:
                return finish()
            off = 0 if kind == "d" else 4
            kts = []
            for J in range(16):
                kts.append(dict(
                    nk=128, col0=0, nd=0, const=-(2048 - 128 * J),
                    kap=lambda idx, J=J, off=off: kTc.ap[:, idx - off, 128 * J:128 * J + 128],
                    ktoks=lambda idx: [kTc.tok()],
                    qtoks=lambda idx: [(idx, 8)],
                    v=(lambda a, hh=None, J=J: vC.ap[:, J, 128 * a:128 * a + 128] if hh is None
                       else vC.ap[:, J, 64 * (2 * (a - 4) + hh): 64 * (2 * (a - 4) + hh) + 64]),
                    vtok=lambda h: [vC.tok()]))
            kts.append(dict(
                nk=64, col0=0, nd=64, const=0,
                kap=lambda idx, qc0=qc0: kTs.ap[:, idx, qc0:qc0 + 64],
                ktoks=lambda idx: [kTs.tok((idx, 8))],
                qtoks=lambda idx: [(idx, 8)],
                v=(lambda a, hh=None, b=b: vBs.ap[:64, b, 128 * a:128 * a + 128] if hh is None
                   else vBs.ap[:64, b, 512 + 64 * (2 * (a - 4) + hh): 512 + 64 * (2 * (a - 4) + hh) + 64]),
                vtok=lambda h, b=b: [vBs.tok((b, h))]))
            if kind == "d":
                run_units(64, qTs, qc0, kts, [("d", h) for h in range(4)])
            else:
                run_units(64, qTs, qc0, kts, [("s", j) for j in range(4)])
            if n_ + 2 < 4:
                cache_load(n_ + 2)
            if n_ == 1:
                dma("pool", wup0.ap, w_up_v[:, :, 0:1024], [], [wup0.tok()])
            if n_ == 2:
                A.release(ksts[0])
                A.release(vCs[0])
                for blk_ in (1, 2):
                    wb_ = A.alloc(8 * 1024 * 2, BF16, "p (k n) -> p k n", k=8)
                    dma("pool", wb_.ap, w_up_v[:, :, blk_ * 1024:(blk_ + 1) * 1024], [], [wb_.tok()])
                    wupb[blk_] = wb_
        if stop == 2.3:
            return finish()
        wout_post([td], 64, gab, junk)

    for bfr in [ksts[1], vCs[1]] + [kTc, qTs, kTs, vBs, wout, gab, mixT] + ering.bufs + fring.bufs + ftmp + Rb + xring.bufs + tring.bufs + qzr.bufs:
        A.release(bfr)

    if stop <= 3:
        return finish()
    wdn = A.alloc(32 * 1024 * 2, BF16, "p (k n) -> p k n", k=32)
    wupb[3] = A.alloc(8 * 1024 * 2, BF16, "p (k n) -> p k n", k=8)
    w_dn_v = w_down.rearrange("(k p) n -> p k n", p=128)
    dma("pool", wupb[3].ap, w_up_v[:, :, 3072:4096], [], [wupb[3].tok()])
    for blk in range(4):
        dma("pool", wdn.ap[:, 8 * blk:8 * blk + 8, :], w_dn_v[:, 8 * blk:8 * blk + 8, :], [], [wdn.tok(blk)])
    gab2 = A.alloc(4096, F32)
    gab = gab2
    x1ring = Ring([A.alloc(4096, F32) for _ in range(4)])
    xsring = Ring([A.alloc(4096, F32) for _ in range(1)])
    tring = Ring([A.alloc(4096, F32) for _ in range(2)])
    h2ring = Ring([A.alloc(8 * 256 * 2, BF16, "p (c t) -> p c t", c=8) for _ in range(2)])
    hid = A.alloc(32 * 256 * 2, BF16, "p (c t) -> p c t", c=32)
    rl = Ring([A.alloc(2048, F32) for _ in range(2)])

    def load_x1(td, xt):
        ti = tiles.index(td)
        dma("sp", xt.ap[:td["P"], :], td["y"][td["row"]:td["row"] + td["P"], :], [ytok[ti]], [xt.tok()])

    make_bcast(5, 0, 128, gab2)

    def ffn_up(h2, N):
        ub = Ring([4, 5])
        for fp in range(16):
            b = ub.next()
            for q in range(2):
                fc = 2 * fp + q
                wsrc = wupb[fc // 8]
                wtok, wc = wsrc.tok(), (fc % 8) * 128
                for kc in range(8):
                    mm(psum[:, b, 256 * q:256 * q + N], wsrc.ap[:, kc, wc:wc + 128], h2.ap[:, kc, 0:N],
                       kc == 0, kc == 7, [wtok, h2.tok()], [PB[b]])
            r = rl.next()
            pv = psum[:, b, :].rearrange("p (q t) -> p q t", q=2)[:, :, 0:N]
            rv = r.ap.rearrange("p (q t) -> p q t", q=2)[:, :, 0:N]
            act(rv, pv, AF.Relu, [PB[b]], [r.tok()])
            tt("dve", hid.ap[:, 2 * fp:2 * fp + 2, 0:N], rv, rv, ALU.mult, [r.tok()], [hid.tok()])

    def ffn_down(gt, keep):
        o = 0
        for td, xt in zip(gt, keep):
            P = td["P"]
            ti = tiles.index(td)
            if td["s"] > 0:
                make_bcast(5, td["s"], 64, gab2)
            for half in range(2):
                for fc in range(32):
                    mm(psum[:P, 6 + half, :], hid.ap[:, fc, o:o + P], wdn.ap[:, fc, half * 512:half * 512 + 512],
                       fc == 0, fc == 31, [hid.tok(), wdn.tok(fc // 8)], [PB[6 + half]])
            post_add(td, ti, (6, 7), gab2, xt, td["y"], None, junk)
            o += P

    h2 = h2ring.next()
    keep = []
    cur = (h2, prenorm_group(groups[0], load_x1, h2, 3, 4, x1ring, xsring, junk, keep), keep)
    for g, gt in enumerate(groups):
        h2, N, keep = cur
        ffn_up(h2, N)
        if g + 1 < len(groups):
            h2n = h2ring.next()
            keepn = []
            cur = (h2n, prenorm_group(groups[g + 1], load_x1, h2n, 3, 4, x1ring, xsring, junk, keepn), keepn)
        ffn_down(gt, keep)

    return finish()


_CACHE = {}


def _prep_inputs(inp):
    constF, constB, alibi = make_consts()
    f = lambda a: np.ascontiguousarray(np.asarray(a, dtype=np.float32))
    w_ada, w_in, w_out = f(inp["w_ada"][0]), f(inp["w_in"][0]), f(inp["w_out"][0])
    w_up, w_down = f(inp["w_up"][0]), f(inp["w_down"][0])
    colp = lambda v, n: f(v).reshape(n, 128).T
    lam = np.concatenate([np.broadcast_to(f(inp[k][0])[None, :], (128, 64))
                          for k in ("lambda_q1", "lambda_k1", "lambda_q2", "lambda_k2")], axis=1)
    cols = np.concatenate([colp(inp["b_ada"][0], 48), colp(inp["g_pre_mix"][0], 8), colp(inp["g_post_mix"][0], 8),
                           colp(inp["g_pre_ffn"][0], 8), colp(inp["g_post_ffn"][0], 8),
                           f(inp["diff_subln_g"][0]).reshape(128, 1), lam, make_kbias()], axis=1)
    cols = f(cols)
    maps = []
    for c in range(8):
        c3 = np.stack([inp["c_prompt"][c], inp["c_sample"][2 * c], inp["c_sample"][2 * c + 1]])
        m = dict(
            xp=f(inp["x_prompt"][c]), xs=f(inp["x_sample"][2 * c:2 * c + 2]).reshape(128, D),
            cT=f(f(c3).reshape(3, 8, 128).transpose(2, 1, 0)),
            cdk=f(inp["cache_diff_k"][0, 2 * c:2 * c + 2]).reshape(2, PAST, 512),
            cdv=f(inp["cache_diff_v"][0, 2 * c:2 * c + 2]).reshape(2, PAST, 512),
            csk=f(inp["cache_sb_k"][0, 2 * c:2 * c + 2]).reshape(2, PAST, 512),
            csv=f(inp["cache_sb_v"][0, 2 * c:2 * c + 2]).reshape(2, PAST, 512),
            w_ada=w_ada, w_in=w_in, w_out=w_out, w_up=w_up, w_down=w_down,
            cols=cols, constF=constF, constB=constB, alibi=alibi)
        maps.append(m)
    return maps


def kernel(**inp):
    if "nc" not in _CACHE:
        _CACHE["nc"], _CACHE["info"] = build_program()
    nc = _CACHE["nc"]
    maps = _prep_inputs(inp)
    res = run_bass_kernel_spmd(nc, maps, core_ids=list(range(8)))
    R = res.results
    y_prompt = np.stack([R[c]["yp"] for c in range(8)])
    y_sample = np.concatenate([R[c]["ys"].reshape(2, NS, D) for c in range(8)])
    outs = [y_prompt, y_sample]
    for n, hd, dd in (("kdp", 4, 128), ("vdp", 4, 128), ("ksp", 8, 64), ("vsp", 8, 64)):
        outs.append(np.stack([R[c][n].reshape(SEQ, hd, dd) for c in range(8)])[None])
    for n, hd, dd in (("kds", 4, 128), ("vds", 4, 128), ("kss", 8, 64), ("vss", 8, 64)):
        outs.append(np.concatenate([R[c][n].reshape(2, NS, hd, dd) for c in range(8)])[None])
    return tuple(np.ascontiguousarray(o, dtype=np.float32) for o in outs)
```
